# Optimizing a Trainium2 kernel written in Bass

```python
import math
import jax
import jax.numpy as jnp
from jax import lax
import numpy as np

D_MODEL = 2048
BATCH = 8
SEQ = 2048
DEPTH = 2

GRID_W = 64
CTX_LEN = 256
EPS = 1e-6
ROPE_THETA = 10000.0
N_MOD = 9
FFN_RESIDUAL = 0.5
D_FF = 256 * ((8 * D_MODEL // 3 + 255) // 256)
N_BRANCH = 3
BRANCH_WIDTH = D_MODEL // 2
Q_BLOCK = 128

MLA_NOPE = 128
MLA_ROPE = 64
MLA_V = 128
MLA_QK_DIM = MLA_NOPE + MLA_ROPE
MLA_HEADS = BRANCH_WIDTH // MLA_V
MLA_KV_RANK = 512

SSD_HEADDIM = 64
SSD_INNER = BRANCH_WIDTH
SSD_HEADS = SSD_INNER // SSD_HEADDIM
SSD_GROUPS = 4
SSD_STATE = 128
SSD_CONV = 5
SSD_CHUNK = 128
SSD_XBC = SSD_INNER + 2 * SSD_GROUPS * SSD_STATE

NA_HEADDIM = 128
NA_HEADS = BRANCH_WIDTH // NA_HEADDIM
NA_WIN_ROWS = 8
NA_WIN_COLS = 16
NA_QBLOCK_COLS = 16
NA_BAND_COLS = NA_QBLOCK_COLS + NA_WIN_COLS

IN_SIZES = (MLA_HEADS * MLA_QK_DIM, MLA_KV_RANK, MLA_ROPE, SSD_INNER, SSD_XBC, 2 * SSD_HEADS, 3 * NA_HEADS * NA_HEADDIM, N_BRANCH * D_MODEL)
D_IN = sum(IN_SIZES)

kernel_name = 'hybrid_dit_mla_ssd_natten'


def rmsnorm(x, g):
    xf = x.astype(jnp.float32)
    y = xf * lax.rsqrt(jnp.mean(xf * xf, axis=-1, keepdims=True) + EPS)
    return (y * g.astype(jnp.float32)).astype(x.dtype)


def adaln(cond, w, b):
    return (jax.nn.silu(cond) @ w + b).reshape(cond.shape[0], N_MOD, D_MODEL)


def modulate(y, m, base):
    return y * (1.0 + m[:, base + 1][:, None]) + m[:, base][:, None]


def swiglu(u, w_gate, w_up, w_down):
    return (jax.nn.silu(u @ w_gate) * (u @ w_up)) @ w_down


def ffn_sublayer(h, m, base, g, w_gate, w_up, w_down):
    u = modulate(rmsnorm(h, g), m, base)
    return h + FFN_RESIDUAL * m[:, base + 2][:, None] * swiglu(u, w_gate, w_up, w_down)


def axial_rope_tables(n_tokens):
    t = jnp.arange(n_tokens)
    pos = jnp.stack([t // GRID_W, t % GRID_W], axis=-1).astype(jnp.float32)
    n_freq = MLA_ROPE // 4
    inv_freq = ROPE_THETA ** (-jnp.arange(n_freq, dtype=jnp.float32) / n_freq)
    ang = pos[:, :, None] * inv_freq
    return jnp.cos(ang), jnp.sin(ang)


def apply_axial_rope(x, cos, sin):
    shp = x.shape
    xr = x.reshape(shp[:-1] + (2, 2, shp[-1] // 4))
    x1, x2 = xr[..., 0, :], xr[..., 1, :]
    c = cos[:, None].astype(x.dtype)
    s = sin[:, None].astype(x.dtype)
    out = jnp.stack([x1 * c - x2 * s, x2 * c + x1 * s], axis=-2)
    return out.reshape(shp)


def blocked_attention(q, k, v):
    bsz, t, h, dq = q.shape
    scale = dq ** -0.5
    q_blocks = jnp.moveaxis(q.reshape(bsz, t // Q_BLOCK, Q_BLOCK, h, dq), 1, 0)

    def one_block(qb):
        logits = jnp.einsum('bqhd,bkhd->bhqk', qb, k).astype(jnp.float32) * scale
        p = jax.nn.softmax(logits, axis=-1).astype(v.dtype)
        return jnp.einsum('bhqk,bkhd->bqhd', p, v)

    o = lax.map(one_block, q_blocks)
    return jnp.moveaxis(o, 0, 1).reshape(bsz, t, h * v.shape[-1])


def mla_qkv(q_raw, ckv_raw, kr_raw, kv_norm_g, w_uk, w_uv, q_norm_g, k_norm_g, rope):
    bsz, t, _ = q_raw.shape
    q = rmsnorm(q_raw.reshape(bsz, t, MLA_HEADS, MLA_QK_DIM), q_norm_g)
    ckv = rmsnorm(ckv_raw, kv_norm_g)
    k_nope = (ckv @ w_uk).reshape(bsz, t, MLA_HEADS, MLA_NOPE)
    v = (ckv @ w_uv).reshape(bsz, t, MLA_HEADS, MLA_V)
    k_rope = jnp.broadcast_to(kr_raw[:, :, None, :], (bsz, t, MLA_HEADS, MLA_ROPE))
    k = rmsnorm(jnp.concatenate([k_nope, k_rope], axis=-1), k_norm_g)
    if rope is not None:
        cos, sin = rope
        q = jnp.concatenate([q[..., :MLA_NOPE], apply_axial_rope(q[..., MLA_NOPE:], cos, sin)], axis=-1)
        k = jnp.concatenate([k[..., :MLA_NOPE], apply_axial_rope(k[..., MLA_NOPE:], cos, sin)], axis=-1)
    return q, k, v


def mla_mixer(lat, ctx, kv_norm_g, w_uk, w_uv, q_norm_g, k_norm_g, cos, sin, need_ctx):
    q_l, k_l, v_l = mla_qkv(*lat, kv_norm_g, w_uk, w_uv, q_norm_g, k_norm_g, (cos, sin))
    q_c, k_c, v_c = mla_qkv(*ctx, kv_norm_g, w_uk, w_uv, q_norm_g, k_norm_g, None)
    out_l = blocked_attention(q_l, jnp.concatenate([k_c, k_l], axis=1), jnp.concatenate([v_c, v_l], axis=1))
    out_c = blocked_attention(q_c, k_c, v_c) if need_ctx else None
    return out_l, out_c


def centred_depthwise_conv(x, w, b):
    pad = SSD_CONV // 2
    y = lax.conv_general_dilated(x, w[:, None, :], window_strides=(1,), padding=((pad, pad),),
                                 dimension_numbers=('NWC', 'WIO', 'NWC'), feature_group_count=x.shape[-1])
    return y + b


def ssd_chunked(x, dt_a, b_mat, c_mat, init_state):
    bsz, t, h, p = x.shape
    g, n = b_mat.shape[2], b_mat.shape[3]
    r = h // g
    q = SSD_CHUNK
    nc = t // q
    xc = x.astype(jnp.float32).reshape(bsz, nc, q, g, r, p)
    ac = dt_a.astype(jnp.float32).reshape(bsz, nc, q, g, r)
    bc = b_mat.astype(jnp.float32).reshape(bsz, nc, q, g, n)
    cc = c_mat.astype(jnp.float32).reshape(bsz, nc, q, g, n)
    a_cum = jnp.cumsum(ac, axis=2)
    causal = jnp.tril(jnp.ones((q, q), dtype=bool))
    seg = a_cum[:, :, :, None] - a_cum[:, :, None, :]
    decay_in = jnp.exp(jnp.where(causal[None, None, :, :, None, None], seg, -jnp.inf))
    cb = jnp.einsum('bclgn,bcsgn->bclsg', cc, bc)
    y_diag = jnp.einsum('bclsgr,bcsgrp->bclgrp', cb[..., None] * decay_in, xc)
    decay_to_end = jnp.exp(a_cum[:, :, -1:] - a_cum)
    chunk_states = jnp.einsum('bclgn,bclgrp->bcgrpn', bc, xc * decay_to_end[..., None])
    chunk_decay = jnp.exp(a_cum[:, :, -1])

    def step(state, inp):
        dec, st = inp
        return state * dec[..., None, None] + st, state

    final, entering = lax.scan(step, init_state.astype(jnp.float32),
                               (jnp.moveaxis(chunk_decay, 1, 0), jnp.moveaxis(chunk_states, 1, 0)))
    entering = jnp.moveaxis(entering, 0, 1)
    y_off = jnp.einsum('bclgn,bcgrpn->bclgrp', cc, entering) * jnp.exp(a_cum)[..., None]
    return (y_diag + y_off).reshape(bsz, t, h, p), final


def ssd_inputs(xbc, dt_raw, conv_w, conv_b, dt_bias):
    bsz, t, _ = xbc.shape
    xbc = jax.nn.silu(centred_depthwise_conv(xbc, conv_w, conv_b))
    xs, bm, cm = jnp.split(xbc, [SSD_INNER, SSD_INNER + SSD_GROUPS * SSD_STATE], axis=-1)
    xs = xs.reshape(bsz, t, SSD_HEADS, SSD_HEADDIM)
    bm = bm.reshape(bsz, t, SSD_GROUPS, SSD_STATE)
    cm = cm.reshape(bsz, t, SSD_GROUPS, SSD_STATE)
    dt = jax.nn.softplus(dt_raw.astype(jnp.float32).reshape(bsz, t, 2, SSD_HEADS) + dt_bias.astype(jnp.float32))
    return xs, bm, cm, dt


def ssd_direction(xs, bm, cm, dt, a, d, init, reverse):
    xdt = xs.astype(jnp.float32) * dt[:, :, d, :, None]
    dta = dt[:, :, d] * a[d]
    if reverse:
        xdt, dta, bm, cm = (jnp.flip(v, axis=1) for v in (xdt, dta, bm, cm))
    y, final = ssd_chunked(xdt, dta, bm, cm, init)
    if reverse:
        y = jnp.flip(y, axis=1)
    return y, final


def ssd_output(y, xs, z, d_skip, norm_g):
    bsz, t = z.shape[:2]
    y = y + d_skip.astype(jnp.float32)[:, None] * xs.astype(jnp.float32)
    v = y.reshape(bsz, t, SSD_INNER) * jax.nn.silu(z.astype(jnp.float32))
    v = v.reshape(bsz, t, SSD_GROUPS, SSD_INNER // SSD_GROUPS)
    v = v * lax.rsqrt(jnp.mean(v * v, axis=-1, keepdims=True) + EPS)
    return (v.reshape(bsz, t, SSD_INNER) * norm_g.astype(jnp.float32)).astype(z.dtype)


def ssd_mixer(lat, ctx, conv_w, conv_b, a_log, dt_bias, d_skip, norm_g, need_ctx):
    z_l, xbc_l, dtr_l = lat
    z_c, xbc_c, dtr_c = ctx
    xs_l, b_l, c_l, dt_l = ssd_inputs(xbc_l, dtr_l, conv_w, conv_b, dt_bias)
    xs_c, b_c, c_c, dt_c = ssd_inputs(xbc_c, dtr_c, conv_w, conv_b, dt_bias)
    a = -jnp.exp(a_log.astype(jnp.float32))
    zero = jnp.zeros((xs_c.shape[0], SSD_GROUPS, SSD_HEADS // SSD_GROUPS, SSD_HEADDIM, SSD_STATE), jnp.float32)
    y_cf, s_f = ssd_direction(xs_c, b_c, c_c, dt_c, a, 0, zero, False)
    y_cb, s_b = ssd_direction(xs_c, b_c, c_c, dt_c, a, 1, zero, True)
    y_lf, _ = ssd_direction(xs_l, b_l, c_l, dt_l, a, 0, s_f, False)
    y_lb, _ = ssd_direction(xs_l, b_l, c_l, dt_l, a, 1, s_b, True)
    out_l = ssd_output(y_lf + y_lb, xs_l, z_l, d_skip, norm_g)
    out_c = ssd_output(y_cf + y_cb, xs_c, z_c, d_skip, norm_g) if need_ctx else None
    return out_l, out_c


def na_heads(qkv, q_norm_g, k_norm_g):
    bsz, t, _ = qkv.shape
    qkv = qkv.reshape(bsz, t, 3, NA_HEADS, NA_HEADDIM)
    return rmsnorm(qkv[:, :, 0], q_norm_g), rmsnorm(qkv[:, :, 1], k_norm_g), qkv[:, :, 2]


def na_mixer(qkv_l, qkv_c, q_norm_g, k_norm_g, rpb, need_ctx):
    q_l, k_l, v_l = na_heads(qkv_l, q_norm_g, k_norm_g)
    q_c, k_c, v_c = na_heads(qkv_c, q_norm_g, k_norm_g)
    bsz, seq = q_l.shape[:2]
    rows = seq // GRID_W
    kr = min(NA_WIN_ROWS, rows)
    scale = NA_HEADDIM ** -0.5
    r = np.arange(rows)
    row_idx = np.clip(r - kr // 2, 0, rows - kr)[:, None] + np.arange(kr)
    row_off = row_idx - r[:, None] + (NA_WIN_ROWS - 1)
    n_cb = GRID_W // NA_QBLOCK_COLS
    q_cols = np.arange(GRID_W).reshape(n_cb, NA_QBLOCK_COLS)
    band_cols = np.clip(q_cols[:, 0] - NA_WIN_COLS // 2, 0, GRID_W - NA_BAND_COLS)[:, None] + np.arange(NA_BAND_COLS)
    win_start = np.clip(q_cols - NA_WIN_COLS // 2, 0, GRID_W - NA_WIN_COLS)
    kc = band_cols[:, None, :]
    col_mask = (kc >= win_start[:, :, None]) & (kc < win_start[:, :, None] + NA_WIN_COLS)
    col_off = np.clip(kc - q_cols[:, :, None] + NA_WIN_COLS - 1, 0, 2 * NA_WIN_COLS - 2)
    rpb_cols = rpb[:, :, col_off]
    kg = k_l.reshape(bsz, rows, GRID_W, NA_HEADS, NA_HEADDIM)
    vg = v_l.reshape(bsz, rows, GRID_W, NA_HEADS, NA_HEADDIM)
    q_rows = jnp.moveaxis(q_l.reshape(bsz, rows, n_cb, NA_QBLOCK_COLS, NA_HEADS, NA_HEADDIM), 1, 0)
    n_win = kr * NA_BAND_COLS

    def one_row(args):
        qb, r_idx, r_off = args
        k_band = jnp.take(kg, r_idx, axis=1)[:, :, band_cols]
        v_band = jnp.take(vg, r_idx, axis=1)[:, :, band_cols]
        s_win = jnp.einsum('bnqhd,brnkhd->bnhqrk', qb, k_band).astype(jnp.float32) * scale
        bias = jnp.transpose(rpb_cols[:, r_off], (2, 0, 3, 1, 4))
        s_win = jnp.where(col_mask[:, None, :, None, :], s_win + bias, -jnp.inf)
        s_ctx = jnp.einsum('bnqhd,bchd->bnhqc', qb, k_c).astype(jnp.float32) * scale
        logits = jnp.concatenate([s_win.reshape(s_win.shape[:4] + (-1,)), s_ctx], axis=-1)
        p = jax.nn.softmax(logits, axis=-1).astype(v_l.dtype)
        p_win = p[..., :n_win].reshape(s_win.shape)
        return (jnp.einsum('bnhqrk,brnkhd->bnqhd', p_win, v_band)
                + jnp.einsum('bnhqc,bchd->bnqhd', p[..., n_win:], v_c))

    o = lax.map(one_row, (q_rows, jnp.asarray(row_idx), jnp.asarray(row_off)))
    out_l = jnp.moveaxis(o, 0, 1).reshape(bsz, seq, NA_HEADS * NA_HEADDIM)
    out_c = blocked_attention(q_c, k_c, v_c) if need_ctx else None
    return out_l, out_c


def merge_branches(branches, gate_cols, w_branch, w_out):
    bsz, t, _ = gate_cols.shape
    gates = jax.nn.sigmoid(gate_cols.reshape(bsz, t, N_BRANCH, D_MODEL))
    merged = gates[:, :, 0] * (branches[0] @ w_branch[0])
    for i in range(1, N_BRANCH):
        merged = merged + gates[:, :, i] * (branches[i] @ w_branch[i])
    return merged @ w_out


def token_mixers(u, uc, w_in, mla_kv_norm_g, mla_w_uk, mla_w_uv, mla_q_norm_g, mla_k_norm_g,
                 ssd_conv_w, ssd_conv_b, ssd_a_log, ssd_dt_bias, ssd_d, ssd_norm_g,
                 na_q_norm_g, na_k_norm_g, na_rpb, w_branch, w_out, cos, sin, need_ctx):
    split_at = np.cumsum(IN_SIZES)[:-1].tolist()
    pl = jnp.split(u @ w_in, split_at, axis=-1)
    pc = jnp.split(uc @ w_in, split_at, axis=-1)
    mla_l, mla_c = mla_mixer(pl[0:3], pc[0:3], mla_kv_norm_g, mla_w_uk, mla_w_uv, mla_q_norm_g, mla_k_norm_g, cos, sin, need_ctx)
    ssd_l, ssd_c = ssd_mixer(pl[3:6], pc[3:6], ssd_conv_w, ssd_conv_b, ssd_a_log, ssd_dt_bias, ssd_d, ssd_norm_g, need_ctx)
    na_l, na_c = na_mixer(pl[6], pc[6], na_q_norm_g, na_k_norm_g, na_rpb, need_ctx)
    out_l = merge_branches((mla_l, ssd_l, na_l), pl[7], w_branch, w_out)
    out_c = merge_branches((mla_c, ssd_c, na_c), pc[7], w_branch, w_out) if need_ctx else None
    return out_l, out_c


def setup_inputs(seed: int = 0) -> dict:
    key = jax.random.key(seed)
    keys = iter(jax.random.split(key, 40))

    def normal(shape, scale):
        return jax.random.normal(next(keys), shape, jnp.float32) * scale

    def gain(shape):
        return 1.0 + normal(shape, 0.02)

    L = DEPTH
    dt0 = jnp.exp(jax.random.uniform(next(keys), (L, 2, SSD_HEADS), jnp.float32, math.log(1e-3), math.log(1e-1)))
    a_init = jax.random.uniform(next(keys), (L, 2, SSD_HEADS), jnp.float32, 1.0, 16.0)
    return {
        'x': normal((BATCH, SEQ, D_MODEL), 1.0),
        'c': normal((BATCH, D_MODEL), 1.0),
        'ctx': normal((BATCH, CTX_LEN, D_MODEL), 1.0),
        'c_ctx': normal((D_MODEL,), 1.0),
        'w_ada': normal((L, D_MODEL, N_MOD * D_MODEL), 0.5 * D_MODEL ** -0.5),
        'b_ada': normal((L, N_MOD * D_MODEL), 0.02),
        'norm_g': gain((L, 3, D_MODEL)),
        'ffn1_w_gate': normal((L, D_MODEL, D_FF), D_MODEL ** -0.5),
        'ffn1_w_up': normal((L, D_MODEL, D_FF), D_MODEL ** -0.5),
        'ffn1_w_down': normal((L, D_FF, D_MODEL), D_FF ** -0.5),
        'ffn2_w_gate': normal((L, D_MODEL, D_FF), D_MODEL ** -0.5),
        'ffn2_w_up': normal((L, D_MODEL, D_FF), D_MODEL ** -0.5),
        'ffn2_w_down': normal((L, D_FF, D_MODEL), D_FF ** -0.5),
        'w_in': normal((L, D_MODEL, D_IN), D_MODEL ** -0.5),
        'mla_kv_norm_g': gain((L, MLA_KV_RANK)),
        'mla_w_uk': normal((L, MLA_KV_RANK, MLA_HEADS * MLA_NOPE), MLA_KV_RANK ** -0.5),
        'mla_w_uv': normal((L, MLA_KV_RANK, MLA_HEADS * MLA_V), MLA_KV_RANK ** -0.5),
        'mla_q_norm_g': gain((L, MLA_QK_DIM)),
        'mla_k_norm_g': gain((L, MLA_QK_DIM)),
        'ssd_conv_w': normal((L, SSD_CONV, SSD_XBC), SSD_CONV ** -0.5),
        'ssd_conv_b': normal((L, SSD_XBC), 0.02),
        'ssd_a_log': jnp.log(a_init),
        'ssd_dt_bias': dt0 + jnp.log(-jnp.expm1(-dt0)),
        'ssd_d': 1.0 + normal((L, SSD_HEADS), 0.1),
        'ssd_norm_g': gain((L, SSD_INNER)),
        'na_q_norm_g': gain((L, NA_HEADDIM)),
        'na_k_norm_g': gain((L, NA_HEADDIM)),
        'na_rpb': normal((L, NA_HEADS, 2 * NA_WIN_ROWS - 1, 2 * NA_WIN_COLS - 1), 0.1),
        'w_branch': normal((L, N_BRANCH, BRANCH_WIDTH, D_MODEL), BRANCH_WIDTH ** -0.5),
        'w_out': normal((L, D_MODEL, D_MODEL), D_MODEL ** -0.5),
    }


def reference(x, c, ctx, c_ctx, w_ada, b_ada, norm_g, ffn1_w_gate, ffn1_w_up, ffn1_w_down,
              ffn2_w_gate, ffn2_w_up, ffn2_w_down, w_in, mla_kv_norm_g, mla_w_uk, mla_w_uv,
              mla_q_norm_g, mla_k_norm_g, ssd_conv_w, ssd_conv_b, ssd_a_log, ssd_dt_bias, ssd_d,
              ssd_norm_g, na_q_norm_g, na_k_norm_g, na_rpb, w_branch, w_out):
    cos, sin = axial_rope_tables(x.shape[1])
    h, hc = x, ctx
    for l in range(DEPTH):
        need_ctx = l < DEPTH - 1
        m = adaln(c, w_ada[l], b_ada[l])
        mc = adaln(c_ctx[None], w_ada[l], b_ada[l])
        h = ffn_sublayer(h, m, 0, norm_g[l, 0], ffn1_w_gate[l], ffn1_w_up[l], ffn1_w_down[l])
        hc = ffn_sublayer(hc, mc, 0, norm_g[l, 0], ffn1_w_gate[l], ffn1_w_up[l], ffn1_w_down[l])
        u = modulate(rmsnorm(h, norm_g[l, 1]), m, 3)
        uc = modulate(rmsnorm(hc, norm_g[l, 1]), mc, 3)
        mix, mix_c = token_mixers(u, uc, w_in[l], mla_kv_norm_g[l], mla_w_uk[l], mla_w_uv[l],
                                  mla_q_norm_g[l], mla_k_norm_g[l], ssd_conv_w[l], ssd_conv_b[l],
                                  ssd_a_log[l], ssd_dt_bias[l], ssd_d[l], ssd_norm_g[l],
                                  na_q_norm_g[l], na_k_norm_g[l], na_rpb[l], w_branch[l], w_out[l],
                                  cos, sin, need_ctx)
        h = h + m[:, 5][:, None] * mix
        h = ffn_sublayer(h, m, 6, norm_g[l, 2], ffn2_w_gate[l], ffn2_w_up[l], ffn2_w_down[l])
        if need_ctx:
            hc = hc + mc[:, 5][:, None] * mix_c
            hc = ffn_sublayer(hc, mc, 6, norm_g[l, 2], ffn2_w_gate[l], ffn2_w_up[l], ffn2_w_down[l])
    return h
```

```python
import numpy as np
from contextlib import ExitStack
import concourse.bass as bass
import concourse.mybir as mybir
from concourse.bass_utils import run_bass_kernel_spmd

F32 = mybir.dt.float32
BF16 = mybir.dt.bfloat16
ALU = mybir.AluOpType
AF = mybir.ActivationFunctionType
AX = mybir.AxisListType

D = 2048
DFF = 5632
T = 2048
TC = 256
NTOK = 2304
NT = 18
L = 2
DIN = 14432
EPS = 1e-6
NEG = -30000.0


class Buf:
    __slots__ = ("name", "lw", "rd", "rd_dma", "excl")

    def __init__(self, name=""):
        self.name = name
        self.lw = None
        self.rd = {}
        self.rd_dma = []
        self.excl = False


class Op:
    __slots__ = ("eng", "fn", "cdeps", "ddeps", "dma", "sem", "semval", "marked", "count", "seq", "gen")


class Tile:
    def __init__(self, t, name=""):
        self.t = t
        self.buf = Buf(name)

    def __getitem__(self, idx):
        return self.t[idx]


DMA_R = {"sp": 8, "act": 4, "pool": 8}


def _buf(b):
    return b if isinstance(b, Buf) else b.buf


class Sched:
    CE = ("pe", "act", "dve", "pool")
    ALLE = ("pe", "act", "dve", "pool", "sp")

    def __init__(self, nc, es):
        self.nc = nc
        self.psem = {e: es.enter_context(nc.semaphore("ps_" + e)) for e in self.CE}
        self.dsem = {q: [es.enter_context(nc.semaphore("ds_%s%d" % (q, i))) for i in range(r)]
                     for q, r in DMA_R.items()}
        self.dcount = {q: 0 for q in DMA_R}
        self.pcount = {e: 0 for e in self.CE}
        self.waited = {e: {} for e in self.ALLE}
        self.ops = []
        self.barrier_c = {}
        self.barrier_d = []
        self.last_dma = {q: [None] * r for q, r in DMA_R.items()}
        self.seq = 0
        self.gen = 0
        self.total = 0

    def op(self, eng, fn, reads=(), writes=(), dma=False):
        o = Op()
        o.eng = eng
        o.fn = fn
        o.dma = dma
        o.marked = False
        o.count = None
        o.sem = None
        o.semval = None
        o.seq = self.seq
        o.gen = self.gen
        self.seq += 1
        cd = dict(self.barrier_c)
        dl = list(self.barrier_d)
        gen = self.gen

        def add(d):
            if d.gen != gen:
                return
            if d.dma:
                dl.append(d)
            else:
                if d.eng == "pe" and eng == "pe" and not dma:
                    return
                cur = cd.get(d.eng)
                if cur is None or cur.seq < d.seq:
                    cd[d.eng] = d

        for b in reads:
            b = _buf(b)
            if b.lw is not None:
                add(b.lw)
            if b.excl:
                for e2, d in b.rd.items():
                    if e2 != eng:
                        add(d)
        for b in writes:
            b = _buf(b)
            if b.lw is not None:
                add(b.lw)
            for d in b.rd.values():
                add(d)
            for d in b.rd_dma:
                add(d)
        if dma:
            q = eng
            i = self.dcount[q]
            self.dcount[q] = i + 1
            r = DMA_R[q]
            slot = i % r
            o.sem = self.dsem[q][slot]
            o.semval = 16 * (i // r + 1)
            prev = self.last_dma[q][slot]
            if prev is not None:
                dl.append(prev)
            self.last_dma[q][slot] = o
        for d in cd.values():
            d.marked = True
        o.cdeps = cd
        o.ddeps = dl
        for b in reads:
            b = _buf(b)
            if dma:
                b.rd_dma.append(o)
            else:
                b.rd[eng] = o
        for b in writes:
            b = _buf(b)
            b.lw = o
            b.rd = {}
            b.rd_dma = []
        self.ops.append(o)
        return o

    def flush(self):
        nc = self.nc
        ops = self.ops
        self.ops = []
        if not ops:
            return
        self.total += len(ops)
        lastc = {}
        for o in ops:
            if (not o.dma) and o.eng in self.psem:
                lastc[o.eng] = o
        for o in lastc.values():
            o.marked = True
        for o in ops:
            if (not o.dma) and o.marked and o.eng in self.psem:
                self.pcount[o.eng] += 1
                o.count = self.pcount[o.eng]
        per = {e: [] for e in self.ALLE}
        for o in ops:
            per[o.eng].append(o)
        psem = self.psem

        def emit(ename, engobj):
            w = self.waited[ename]
            for o in per[ename]:
                for e, d in o.cdeps.items():
                    key = "p" + e
                    if w.get(key, 0) < d.count:
                        engobj.wait_ge(psem[e], d.count)
                        w[key] = d.count
                for d in o.ddeps:
                    key = d.sem.name
                    if w.get(key, 0) < d.semval:
                        engobj.wait_ge(d.sem, d.semval)
                        w[key] = d.semval
                ins = o.fn(engobj)
                if o.dma:
                    ins.then_inc(o.sem, 16)
                elif o.marked and ename in psem:
                    ins.then_inc(psem[ename], 1)

        with nc.Block() as block:
            if per["sp"]:
                block.sync(lambda e: emit("sp", e))
            if per["pe"]:
                block.tensor(lambda e: emit("pe", e))
            if per["act"]:
                block.scalar(lambda e: emit("act", e))
            if per["dve"]:
                block.vector(lambda e: emit("dve", e))
            if per["pool"]:
                block.gpsimd(lambda e: emit("pool", e))
        for e, o in lastc.items():
            self.barrier_c[e] = o
        bd = []
        for q in DMA_R:
            bd.extend(o for o in self.last_dma[q] if o is not None)
        self.barrier_d = bd
        self.gen += 1

    def finish(self):
        for e in ("sp", "act", "dve", "pool"):
            self.op(e, lambda g: g.nop())
        self.flush()


class KB:
    def __init__(self, nc, es, dbg):
        self.nc = nc
        self.es = es
        self.S = Sched(nc, es)
        self.dbg = set(dbg)
        self.pes = None
        self.cnt = 0
        self.pb = [Tile(es.enter_context(nc.psum_tensor("pb%d" % i, [128, 512], F32)), "pb%d" % i) for i in range(8)]
        for b in self.pb:
            b.buf.excl = True
        self.rot = list(range(8))
        self.roti = 0

    def name(self, p):
        self.cnt += 1
        return "%s_%d" % (p, self.cnt)

    def phase(self):
        kb = self

        class _P:
            def __enter__(s):
                kb.pes = ExitStack()
                kb.pes.__enter__()
                kb.rot = list(range(8))
                kb.roti = 0
                return kb

            def __exit__(s, *a):
                if a[0] is None:
                    kb.S.flush()
                kb.pes.__exit__(*a)
                kb.pes = None
                return False
        return _P()

    def sb(self, shape, dt, name="t", glob=False):
        st = self.es if glob else self.pes
        n = self.name(name)
        return Tile(st.enter_context(self.nc.sbuf_tensor(n, list(shape), dt)), n)

    def ring(self, n, shape, dt, name="r"):
        return [self.sb(shape, dt, name) for _ in range(n)]

    def dram(self, name, shape, dt, kind=None):
        if kind is None:
            kind = "ExternalOutput" if name in self.dbg else "Internal"
        return self.nc.dram_tensor(name, list(shape), dt, kind=kind).ap()

    def bank(self):
        b = self.pb[self.rot[self.roti % len(self.rot)]]
        self.roti += 1
        return b

    def v(self, eng, meth, reads, writes, **kw):
        self.S.op(eng, lambda e: getattr(e, meth)(**kw), reads, writes)

    def dma(self, q, out, in_, reads=(), writes=()):
        self.S.op(q, lambda e: e.dma_start(out=out, in_=in_), reads, writes, dma=True)

    def mm(self, ot, out, lt, lhsT, rt, rhs, start, stop):
        self.S.op("pe", lambda e: e.matmul(out, lhsT=lhsT, rhs=rhs, start=start, stop=stop), [lt, rt], [ot])

    def tr(self, ot, out, it, in_, idt):
        self.S.op("pe", lambda e: e.transpose(out=out, in_=in_, identity=idt[:]), [it, idt], [ot])


def bcast_rows(ap, n):
    aps = [list(x) for x in ap.ap]
    while len(aps) > 1 and aps[0][1] == 1:
        aps = aps[1:]
    return bass.AP(ap.tensor, ap.offset, [[0, n]] + aps)


def bf(bank):
    return bank[:].bitcast(BF16)


def phase_consts(K, C):
    C.ident = K.sb([128, 128], BF16, "ident", glob=True)
    C.identf = K.sb([128, 128], F32, "identf", glob=True)
    C.ones = K.sb([128, 128], BF16, "ones", glob=True)
    C.onesf = K.sb([128, 128], F32, "onesf", glob=True)
    with K.phase():
        K.dma("sp", C.identf[:], C.cst[:, 0, :], [], [C.identf])
        K.v("dve", "tensor_copy", [C.identf], [C.ident], out=C.ident[:], in_=C.identf[:])
        K.v("pool", "memset", [], [C.onesf], ap=C.onesf[:], constant=1.0)
        K.v("pool", "memset", [], [C.ones], ap=C.ones[:], constant=1.0)


def phase_adaln(K, C, l):
    with K.phase():
        wb = K.ring(2, [128, 16, 512], F32, "wada")
        msb = K.sb([2, 18432], F32, "msb")
        gsb = K.sb([2, 3 * D], F32, "gsb")
        sc = K.sb([128, 16, 2], F32, "sc")
        K.dma("sp", sc[:], C.cT[:, :, :], [], [sc])
        K.v("act", "activation", [sc], [sc], out=sc[:], in_=sc[:], func=AF.Silu)
        K.dma("sp", msb[:], bcast_rows(C.b_ada[l, :], 2), [], [msb])
        K.dma("sp", gsb[:], bcast_rows(C.norm_g[l].rearrange("a d -> (a d)"), 2), [], [gsb])
        wv = C.w_ada[l].rearrange("(kc p) n -> p kc n", p=128)
        for nb in range(36):
            w = wb[nb % 2]
            K.dma("sp", w[:, 0:8, :], wv[:, 0:8, nb * 512:(nb + 1) * 512], [], [w])
            K.dma("act", w[:, 8:16, :], wv[:, 8:16, nb * 512:(nb + 1) * 512], [], [w])
            bk = K.bank()
            for kc in range(16):
                K.mm(bk, bk[0:2, :], sc, sc[:, kc, :], w, w[:, kc, :], kc == 0, kc == 15)
            c0 = nb * 512
            K.v("dve", "tensor_tensor", [bk, msb], [msb], out=msb[:, c0:c0 + 512], in0=bk[0:2, :],
                in1=msb[:, c0:c0 + 512], op=ALU.add)
        for s in range(3):
            a = msb[:, (3 * s + 1) * D:(3 * s + 2) * D]
            K.v("dve", "scalar_tensor_tensor", [msb, gsb], [msb], out=a, in0=a, scalar=1.0,
                in1=gsb[:, s * D:(s + 1) * D], op0=ALU.add, op1=ALU.mult)
        for s in (0, 2):
            g = msb[:, (3 * s + 2) * D:(3 * s + 3) * D]
            K.v("dve", "tensor_scalar", [msb], [msb], out=g, in0=g, scalar1=0.5, scalar2=None, op0=ALU.mult)
        K.dma("sp", C.modr[l], msb[:], [msb], [])


def gen_adaln_blocks(K, C, l, b0, b1):
    wb = K.ring(2, [128, 16, 512], F32, "wadas")
    sc = K.sb([128, 16, 2], F32, "scs")
    bb = K.ring(2, [2, 512], F32, "bbs")
    bt2 = K.ring(2, [2, 512], F32, "bt2s")
    K.dma("pool", sc[:], C.cT[:, :, :], [], [sc])
    K.v("act", "activation", [sc], [sc], out=sc[:], in_=sc[:], func=AF.Silu)
    wv = C.w_ada[l].rearrange("(kc p) n -> p kc n", p=128)
    for nb in range(b0, b1):
        w = wb[nb % 2]
        b_ = bb[nb % 2]
        c0 = nb * 512
        K.dma("pool", w[:], wv[:, :, c0:c0 + 512], [], [w])
        K.dma("pool", b_[:], bcast_rows(C.b_ada[l, c0:c0 + 512], 2), [], [b_])
        yield
        bk = K.bank()
        for kc in range(16):
            K.mm(bk, bk[0:2, :], sc, sc[:, kc, :], w, w[:, kc, :], kc == 0, kc == 15)
        t2 = bt2[nb % 2]
        K.v("act", "copy", [bk], [t2], out=t2[:], in_=bk[0:2, :])
        K.v("pool", "tensor_tensor", [t2, b_], [b_], out=b_[:], in0=t2[:], in1=b_[:], op=ALU.add)
        K.dma("pool", C.modraw[l][:, c0:c0 + 512], b_[:], [b_], [])
        yield


def phase_adaln_finish(K, C, l):
    with K.phase():
        msb = K.sb([2, 18432], F32, "msbf")
        gsb = K.sb([2, 3 * D], F32, "gsbf")
        K.dma("sp", msb[:], C.modraw[l], [], [msb])
        K.dma("sp", gsb[:], bcast_rows(C.norm_g[l].rearrange("a d -> (a d)"), 2), [], [gsb])
        for s in range(3):
            a = msb[:, (3 * s + 1) * D:(3 * s + 2) * D]
            K.v("dve", "scalar_tensor_tensor", [msb, gsb], [msb], out=a, in0=a, scalar=1.0,
                in1=gsb[:, s * D:(s + 1) * D], op0=ALU.add, op1=ALU.mult)
        for s in (0, 2):
            g = msb[:, (3 * s + 2) * D:(3 * s + 3) * D]
            K.v("dve", "tensor_scalar", [msb], [msb], out=g, in0=g, scalar1=0.5, scalar2=None, op0=ALU.mult)
        K.dma("sp", C.modr[l], msb[:], [msb], [])


def load_mod(K, C, l, r, j, tile):
    src = C.modr[l][r:r + 1, j * D:(j + 1) * D]
    K.dma("sp", tile[:], bcast_rows(src, 128), [], [tile])


def norm_mod_T(K, C, srcs, Al, Bl, uT, W):
    for i, src in enumerate(srcs):
        A = Al[i]
        Bm = Bl[i]
        hb = W.hb[i % len(W.hb)]
        K.dma("sp", hb[:], src, [], [hb])
        ss = W.st[i % 2]
        K.v("act", "activation", [hb], [W.junk, ss], out=W.junk[:], in_=hb[:], func=AF.Square,
            scale=float(D) ** -0.5, accum_out=ss[:, 0:1])
        K.v("act", "activation", [ss], [ss], out=ss[:, 1:2], in_=ss[:, 0:1], func=AF.Sqrt, bias=EPS, scale=1.0)
        K.v("dve", "reciprocal", [ss], [ss], out=ss[:, 2:3], in_=ss[:, 1:2])
        tmp = W.tmp
        K.v("dve", "scalar_tensor_tensor", [hb, ss, A], [tmp], out=tmp[:], in0=hb[:], scalar=ss[:, 2:3],
            in1=A[:], op0=ALU.mult, op1=ALU.mult)
        ub = W.ub[i % len(W.ub)]
        K.v("dve", "tensor_tensor", [tmp, Bm], [ub], out=ub[:], in0=tmp[:], in1=Bm[:], op=ALU.add)
        for g in range(2):
            bk = K.bank()
            bv = bf(bk)
            for c in range(8):
                cc = g * 8 + c
                K.tr(bk, bv[:, c * 128:(c + 1) * 128], ub, ub[:, cc * 128:(cc + 1) * 128], C.ident)
            K.v("act" if g == 0 else "dve", "copy" if g == 0 else "tensor_copy", [bk], [uT],
                out=uT[:, g * 8:(g + 1) * 8, i * 128:(i + 1) * 128],
                in_=bv.rearrange("p (c t) -> p c t", c=8))


def split_tok(n):
    if n <= 512:
        return [(0, n)]
    h = n // 2
    return [(0, h), (h, n - h)]


class NS:
    pass


def phase_ffn(K, C, l, s, wg, wu, wd, blocks, src, dst):
    with K.phase():
        W = NS()
        W.hb = K.ring(2, [128, D], F32, "hb")
        W.st = K.ring(2, [128, 4], F32, "st")
        W.ub = K.ring(2, [128, D], BF16, "ub")
        Ar = [K.sb([128, D], F32, "A") for _ in range(2)]
        Br = [K.sb([128, D], F32, "B") for _ in range(2)]
        uT = K.sb([128, 16, 768], BF16, "uT")
        aT = K.sb([128, 44, 768], BF16, "aT")
        wgt = K.ring(2, [128, 16, 128], BF16, "wg")
        wut = K.ring(2, [128, 16, 128], BF16, "wu")
        wdt = K.ring(2, [128, 11, 512], BF16, "wd")
        sg = K.ring(2, [128, 512], F32, "sg")
        hs = K.ring(2, [128, 512], F32, "hs")
        ho = K.ring(2, [128, 512], F32, "ho")
        wgv = wg.rearrange("(kc p) n -> p kc n", p=128)
        wuv = wu.rearrange("(kc p) n -> p kc n", p=128)
        wdv = wd.rearrange("(j p) n -> p j n", p=128)
        W.tmp = K.sb([128, D], F32, "tmp")
        W.junk = W.tmp
        nd = 0
        for tiles in blocks:
            n = len(tiles)
            ntok = n * 128
            rs = sorted(set(1 if t < 2 else 0 for t in tiles))
            for r in rs:
                load_mod(K, C, l, r, 3 * s + 1, Ar[r])
                load_mod(K, C, l, r, 3 * s + 0, Br[r])
            K.rot = list(range(8))
            norm_mod_T(K, C, [src(t) for t in tiles], [Ar[1 if t < 2 else 0] for t in tiles],
                       [Br[1 if t < 2 else 0] for t in tiles], uT, W)
            for r in rs:
                load_mod(K, C, l, r, 3 * s + 2, Ar[r])
            halves = split_tok(ntok)
            for j in range(44):
                a = wgt[j % 2]
                b = wut[j % 2]
                K.dma("pool", a[:], wgv[:, :, j * 128:(j + 1) * 128], [], [a])
                K.dma("pool", b[:], wuv[:, :, j * 128:(j + 1) * 128], [], [b])
                for (t0, nn) in halves:
                    pg = K.bank()
                    pu = K.bank()
                    for kc in range(16):
                        K.mm(pg, pg[:, 0:nn], a, a[:, kc, :], uT, uT[:, kc, t0:t0 + nn], kc == 0, kc == 15)
                    for kc in range(16):
                        K.mm(pu, pu[:, 0:nn], b, b[:, kc, :], uT, uT[:, kc, t0:t0 + nn], kc == 0, kc == 15)
                    sgt = sg[nd % 2]
                    nd += 1
                    K.v("act", "activation", [pg], [sgt], out=sgt[:, 0:nn], in_=pg[:, 0:nn], func=AF.Silu)
                    K.v("dve", "tensor_tensor", [sgt, pu], [aT], out=aT[:, j, t0:t0 + nn], in0=sgt[:, 0:nn],
                        in1=pu[:, 0:nn], op=ALU.mult)
            qi = 0
            for nb in range(4):
                for q in range(4):
                    wt = wdt[qi % 2]
                    qi += 1
                    K.dma("pool", wt[:], wdv[:, q * 11:(q + 1) * 11, nb * 512:(nb + 1) * 512], [], [wt])
                    for i in range(n):
                        bk = K.pb[i]
                        for jj in range(11):
                            j = q * 11 + jj
                            K.mm(bk, bk[:, :], aT, aT[:, j, i * 128:(i + 1) * 128], wt, wt[:, jj, :], j == 0, j == 43)
                for i, t in enumerate(tiles):
                    bk = K.pb[i]
                    h1 = hs[(nb * n + i) % 2]
                    o1 = ho[(nb * n + i) % 2]
                    G = Ar[1 if t < 2 else 0]
                    K.dma("sp", h1[:], src(t)[:, nb * 512:(nb + 1) * 512], [], [h1])
                    K.v("dve", "tensor_tensor", [bk, G], [o1], out=o1[:], in0=bk[:, :], in1=G[:, nb * 512:(nb + 1) * 512],
                        op=ALU.mult)
                    K.v("dve", "tensor_tensor", [o1, h1], [o1], out=o1[:], in0=o1[:], in1=h1[:], op=ALU.add)
                    K.dma("sp", dst(t)[:, nb * 512:(nb + 1) * 512], o1[:], [o1], [])


def rms_groups(K, X, xv, G, Dg, gain_bc, Y, yv, W):
    if isinstance(W.sq, list):
        W.rmsi = getattr(W, "rmsi", 0) + 1
        sq = W.sq[W.rmsi % len(W.sq)]
        st = W.gst[W.rmsi % len(W.gst)]
    else:
        sq = W.sq
        st = W.gst
    sqv = sq[:, 0:G * Dg].rearrange("p (g d) -> p g d", g=G)
    K.v("dve", "tensor_tensor", [X], [sq], out=sqv, in0=xv, in1=xv, op=ALU.mult)
    K.v("dve", "tensor_reduce", [sq], [st], out=st[:, 0:G], in_=sqv, axis=AX.X, op=ALU.add)
    K.v("act", "activation", [st], [st], out=st[:, 8:8 + G], in_=st[:, 0:G], func=AF.Sqrt, bias=EPS, scale=1.0 / Dg)
    K.v("dve", "reciprocal", [st], [st], out=st[:, 16:16 + G], in_=st[:, 8:8 + G])
    K.v("dve", "tensor_tensor", [X, st], [sq], out=sqv, in0=xv,
        in1=st[:, 16:16 + G].unsqueeze(2).to_broadcast([128, G, Dg]), op=ALU.mult)
    K.v("dve", "tensor_tensor", [sq, W.gains], [Y], out=yv, in0=sqv, in1=gain_bc, op=ALU.mult)


def rope_cast(K, Y, yv, H, Ob, ov, cs, W, latent):
    if not latent:
        K.v("act", "copy", [Y], [Ob], out=ov, in_=yv)
        return
    K.v("act", "copy", [Y], [Ob], out=ov[:, :, 0:128], in_=yv[:, :, 0:128])
    yr = yv[:, :, 128:192].rearrange("p h (a t f) -> p h a t f", a=2, t=2)
    orr = ov[:, :, 128:192].rearrange("p h (a t f) -> p h a t f", a=2, t=2)
    x1 = yr[:, :, :, 0, :]
    x2 = yr[:, :, :, 1, :]
    cb = cs[:, 0:32].rearrange("p (a f) -> p a f", a=2).unsqueeze(1).to_broadcast([128, H, 2, 16])
    sb_ = cs[:, 32:64].rearrange("p (a f) -> p a f", a=2).unsqueeze(1).to_broadcast([128, H, 2, 16])
    W.rti = getattr(W, "rti", 0) + 1
    rt = W.rt[W.rti % len(W.rt)] if isinstance(W.rt, list) else W.rt
    tv = [rt[:, k * 256:k * 256 + H * 32].rearrange("p (h a f) -> p h a f", h=H, a=2) for k in range(4)]
    K.v("dve", "tensor_tensor", [Y, cs], [rt], out=tv[0], in0=x1, in1=cb, op=ALU.mult)
    K.v("dve", "tensor_tensor", [Y, cs], [rt], out=tv[1], in0=x2, in1=sb_, op=ALU.mult)
    K.v("dve", "tensor_tensor", [Y, cs], [rt], out=tv[2], in0=x2, in1=cb, op=ALU.mult)
    K.v("dve", "tensor_tensor", [Y, cs], [rt], out=tv[3], in0=x1, in1=sb_, op=ALU.mult)
    K.v("dve", "tensor_tensor", [rt], [Ob], out=orr[:, :, :, 0, :], in0=tv[0], in1=tv[1], op=ALU.subtract)
    K.v("dve", "tensor_tensor", [rt], [Ob], out=orr[:, :, :, 1, :], in0=tv[2], in1=tv[3], op=ALU.add)


def phase_win(K, C, l, blocks, src):
    with K.phase():
        W = NS()
        scr = K.sb([128, 13312], F32, "scr")

        def sub(a, b, dt=F32, name="s"):
            ap = scr.t[:, a:b]
            if dt == BF16:
                ap = ap.bitcast(BF16)
            return Tile(ap, name)
        W.hb = [sub(0, 2048), K.sb([128, D], F32, "hb2")]
        W.tmp = sub(2048, 4096)
        W.junk = W.tmp
        W.ub = [sub(4096, 5120, BF16), K.sb([128, D], BF16, "ub2")]
        Ar = [sub(5120, 7168), sub(9216, 11264)]
        Br = [sub(7168, 9216), sub(11264, 13312)]
        Xs = [sub(i * 1536, (i + 1) * 1536) for i in range(6)]
        W.sq = [sub(9216, 10752), sub(10752, 12288)]
        W.rt = [sub(12288, 13312), K.sb([128, 1024], F32, "rt1")]
        W.st = K.ring(2, [128, 4], F32, "st")
        W.gst = K.ring(2, [128, 24], F32, "gst")
        uT = K.sb([128, 16, 768], BF16, "uT")
        wr = K.ring(2, [128, 16, 512], BF16, "win")
        wuk = K.sb([128, 4, 1024], BF16, "wuk")
        wuv = K.sb([128, 4, 1024], BF16, "wuv")
        gains = K.sb([128, 1184], F32, "gains")
        W.gains = gains
        Yr = K.ring(2, [128, 1536], F32, "Y")
        Obs = [K.sb([128, 1536], BF16, "Ob") for _ in range(6)]
        ckTs = [K.sb([128, 4, 128], BF16, "ckT") for _ in range(6)]
        stg = K.ring(2, [128, 16, 128], BF16, "stg")
        kr = K.ring(6, [128, 64], F32, "kr")
        o32 = K.ring(2, [128, 768], F32, "o32")
        o16 = K.ring(2, [128, 1024], BF16, "o16")
        csr = K.ring(6, [128, 64], F32, "cs")
        sp = K.sb([128, 5, 32], F32, "sp")
        K.dma("sp", gains[:, 0:192], bcast_rows(C.mla_q_norm_g[l, :], 128), [], [gains])
        K.dma("sp", gains[:, 192:384], bcast_rows(C.mla_k_norm_g[l, :], 128), [], [gains])
        K.dma("sp", gains[:, 384:896], bcast_rows(C.mla_kv_norm_g[l, :], 128), [], [gains])
        K.dma("sp", gains[:, 896:1024], bcast_rows(C.na_q_norm_g[l, :], 128), [], [gains])
        K.dma("sp", gains[:, 1024:1152], bcast_rows(C.na_k_norm_g[l, :], 128), [], [gains])
        K.dma("sp", gains[:, 1152:1184], bcast_rows(C.ssd_dt_bias[l].rearrange("a h -> (a h)"), 128), [], [gains])
        K.dma("pool", wuk[:], C.mla_w_uk[l].rearrange("(c p) n -> p c n", p=128), [], [wuk])
        K.dma("pool", wuv[:], C.mla_w_uv[l].rearrange("(c p) n -> p c n", p=128), [], [wuv])
        wv = C.w_in[l].rearrange("(kc p) n -> p kc n", p=128)
        cnt = NS()
        cnt.wi = 0
        cnt.oi = 0

        for tiles in blocks:
            n = len(tiles)
            ntok = n * 128
            rs = sorted(set(1 if t < 2 else 0 for t in tiles))
            for r in rs:
                load_mod(K, C, l, r, 4, Ar[r])
                load_mod(K, C, l, r, 3, Br[r])
            K.rot = list(range(8))
            norm_mod_T(K, C, [src(t) for t in tiles], [Ar[1 if t < 2 else 0] for t in tiles],
                       [Br[1 if t < 2 else 0] for t in tiles], uT, W)
            K.S.flush()
            cst = {}
            for i, t in enumerate(tiles):
                if t >= 2:
                    cst[i] = csr[i % 6]
                    K.dma("sp", cst[i][:], C.rope[(t - 2) * 128:(t - 1) * 128, :], [], [cst[i]])
            pairs = [list(range(a, min(a + 2, n))) for a in range(0, n, 2)]

            def load_w(c0, nc_):
                w = wr[cnt.wi % 2]
                cnt.wi += 1
                K.dma("pool", w[:, :, 0:nc_], wv[:, :, c0:c0 + nc_], [], [w])
                return w

            def proj(w, nc_, i):
                bk = K.bank()
                for kc in range(16):
                    K.mm(bk, bk[:, 0:nc_], uT, uT[:, kc, i * 128:(i + 1) * 128], w, w[:, kc, 0:nc_], kc == 0, kc == 15)
                return bk

            def seg_to_x(c0, nc_, x0):
                w = load_w(c0, nc_)
                for i in range(n):
                    bk = proj(w, nc_, i)
                    K.v("act", "copy", [bk], [Xs[i]], out=Xs[i][:, x0:x0 + nc_], in_=bk[:, 0:nc_])

            def easy(kind, c0, nc_):
                w = load_w(c0, nc_)
                if kind == "xbc":
                    for ci in range(4):
                        o = o32[cnt.oi % 2]
                        cnt.oi += 1
                        for (t0, nn) in split_tok(ntok):
                            bk = K.bank()
                            for kc in range(16):
                                K.mm(bk, bk[:, 0:nn], w, w[:, kc, ci * 128:(ci + 1) * 128], uT, uT[:, kc, t0:t0 + nn], kc == 0, kc == 15)
                            K.v("act", "copy", [bk], [o], out=o[:, t0:t0 + nn], in_=bk[:, 0:nn])
                        ch0 = c0 - 3136 + ci * 128
                        K.dma("sp", C.xbcT[ch0:ch0 + 128, tiles[0] * 128:tiles[0] * 128 + ntok], o[:, 0:ntok], [o], [])
                    return
                for i, t in enumerate(tiles):
                    tok = t * 128
                    bk = proj(w, nc_, i)
                    if kind == "kr":
                        K.v("act", "copy", [bk], [kr[i]], out=kr[i][:], in_=bk[:, 0:64])
                    elif kind in ("z", "g"):
                        o = o32[cnt.oi % 2]
                        cnt.oi += 1
                        K.v("act", "activation", [bk], [o], out=o[:, 0:512], in_=bk[:, :], func=(AF.Silu if kind == "z" else AF.Sigmoid))
                        if kind == "z":
                            K.dma("sp", C.zs[tok:tok + 128, c0 - 2112:c0 - 2112 + 512], o[:, 0:512], [o], [])
                        else:
                            K.dma("sp", C.gts[tok:tok + 128, c0 - 8288:c0 - 8288 + 512], o[:, 0:512], [o], [])
                    elif kind == "nv":
                        o = o16[cnt.oi % 2]
                        cnt.oi += 1
                        K.v("act", "copy", [bk], [o], out=o[:, 0:512], in_=bk[:, :])
                        K.dma("sp", C.nv[tok:tok + 128, c0 - 7264:c0 - 7264 + 512], o[:, 0:512], [o], [])
                    elif kind == "dt":
                        K.v("dve", "tensor_tensor", [bk, gains], [sp], out=sp[:, 0, :], in0=bk[:, 0:32], in1=gains[:, 1152:1184], op=ALU.add)
                        K.v("act", "activation", [sp], [sp], out=sp[:, 1, :], in_=sp[:, 0, :], func=AF.Abs)
                        K.v("act", "activation", [sp], [sp], out=sp[:, 2, :], in_=sp[:, 1, :], func=AF.Exp, scale=-1.0)
                        K.v("act", "activation", [sp], [sp], out=sp[:, 3, :], in_=sp[:, 2, :], func=AF.Ln, bias=1.0, scale=1.0)
                        K.v("dve", "tensor_scalar_max", [sp], [sp], out=sp[:, 1, :], in0=sp[:, 0, :], scalar1=0.0)
                        K.v("dve", "tensor_tensor", [sp], [sp], out=sp[:, 4, :], in0=sp[:, 1, :], in1=sp[:, 3, :], op=ALU.add)
                        K.dma("sp", C.dts[tok:tok + 128, :], sp[:, 4, :], [sp], [])

            def rms_multi(grp, G, Dg, gain_bc, outs):
                xv = [Xs[i][:, 0:G * Dg].rearrange("p (g d) -> p g d", g=G) for i in grp]
                sq = [W.sq[k % 2] for k in range(len(grp))]
                sqv = [q_[:, 0:G * Dg].rearrange("p (g d) -> p g d", g=G) for q_ in sq]
                st = [W.gst[k % 2] for k in range(len(grp))]
                for k, i in enumerate(grp):
                    K.v("dve", "tensor_tensor", [Xs[i]], [sq[k]], out=sqv[k], in0=xv[k], in1=xv[k], op=ALU.mult)
                for k, i in enumerate(grp):
                    K.v("dve", "tensor_reduce", [sq[k]], [st[k]], out=st[k][:, 0:G], in_=sqv[k], axis=AX.X, op=ALU.add)
                for k, i in enumerate(grp):
                    K.v("act", "activation", [st[k]], [st[k]], out=st[k][:, 8:8 + G], in_=st[k][:, 0:G], func=AF.Sqrt, bias=EPS, scale=1.0 / Dg)
                for k, i in enumerate(grp):
                    K.v("dve", "reciprocal", [st[k]], [st[k]], out=st[k][:, 16:16 + G], in_=st[k][:, 8:8 + G])
                for k, i in enumerate(grp):
                    K.v("dve", "tensor_tensor", [Xs[i], st[k]], [sq[k]], out=sqv[k], in0=xv[k],
                        in1=st[k][:, 16:16 + G].unsqueeze(2).to_broadcast([128, G, Dg]), op=ALU.mult)
                for k, i in enumerate(grp):
                    K.v("dve", "tensor_tensor", [sq[k], gains], [outs[k][0]], out=outs[k][1], in0=sqv[k], in1=gain_bc, op=ALU.mult)

            def rope_multi(grp, Ys):
                H = 8
                yv = [y[:, :].rearrange("p (h d) -> p h d", h=H) for y in Ys]
                ov = [Obs[i][:, :].rearrange("p (h d) -> p h d", h=H) for i in grp]
                lat = [tiles[i] >= 2 for i in grp]
                for k, i in enumerate(grp):
                    if lat[k]:
                        K.v("act", "copy", [Ys[k]], [Obs[i]], out=ov[k][:, :, 0:128], in_=yv[k][:, :, 0:128])
                    else:
                        K.v("act", "copy", [Ys[k]], [Obs[i]], out=ov[k], in_=yv[k])
                ks = [k for k in range(len(grp)) if lat[k]]
                if not ks:
                    return
                yr = {k: yv[k][:, :, 128:192].rearrange("p h (a t f) -> p h a t f", a=2, t=2) for k in ks}
                orr = {k: ov[k][:, :, 128:192].rearrange("p h (a t f) -> p h a t f", a=2, t=2) for k in ks}
                cb = {k: cst[grp[k]][:, 0:32].rearrange("p (a f) -> p a f", a=2).unsqueeze(1).to_broadcast([128, H, 2, 16]) for k in ks}
                sb_ = {k: cst[grp[k]][:, 32:64].rearrange("p (a f) -> p a f", a=2).unsqueeze(1).to_broadcast([128, H, 2, 16]) for k in ks}
                rt = {k: W.rt[k % 2] for k in ks}
                tv = {k: [rt[k][:, j * 256:j * 256 + H * 32].rearrange("p (h a f) -> p h a f", h=H, a=2) for j in range(4)] for k in ks}
                for (j, xi, tb) in ((0, 0, cb), (1, 1, sb_), (2, 1, cb), (3, 0, sb_)):
                    for k in ks:
                        K.v("dve", "tensor_tensor", [Ys[k], cst[grp[k]]], [rt[k]], out=tv[k][j], in0=yr[k][:, :, :, xi, :], in1=tb[k], op=ALU.mult)
                for k in ks:
                    K.v("dve", "tensor_tensor", [rt[k]], [Obs[grp[k]]], out=orr[k][:, :, :, 0, :], in0=tv[k][0], in1=tv[k][1], op=ALU.subtract)
                for k in ks:
                    K.v("dve", "tensor_tensor", [rt[k]], [Obs[grp[k]]], out=orr[k][:, :, :, 1, :], in0=tv[k][2], in1=tv[k][3], op=ALU.add)

            def heads_A(g0):
                for grp in pairs:
                    Ys = [Yr[k % 2] for k in range(len(grp))]
                    outs = [(Ys[k], Ys[k][:, :].rearrange("p (h d) -> p h d", h=8)) for k in range(len(grp))]
                    rms_multi(grp, 8, 192, gains[:, g0:g0 + 192].unsqueeze(1).to_broadcast([128, 8, 192]), outs)
                    rope_multi(grp, Ys)

            def heads_B(dstT):
                for i, t in enumerate(tiles):
                    tok = t * 128
                    ov8 = Obs[i][:, :].rearrange("p (h d) -> p h d", h=8)
                    ba = K.bank()
                    bb = K.bank()
                    bav = bf(ba)
                    bbv = bf(bb)
                    for hh in range(8):
                        K.tr(ba, bav[:, hh * 128:(hh + 1) * 128], Obs[i], ov8[:, hh, 0:128], C.ident)
                    for hh in range(8):
                        K.tr(bb, bbv[0:64, hh * 128:(hh + 1) * 128], Obs[i], ov8[:, hh, 128:192], C.ident)
                    sg_ = stg[cnt.oi % 2]
                    cnt.oi += 1
                    K.v("act", "copy", [ba], [sg_], out=sg_[:, 0:8, :], in_=bav.rearrange("p (h t) -> p h t", h=8))
                    K.v("dve", "tensor_copy", [bb], [sg_], out=sg_[0:64, 8:16, :], in_=bbv[0:64, :].rearrange("p (h t) -> p h t", h=8))
                    K.dma("sp", dstT[:, 0:128, tok:tok + 128].rearrange("h d t -> d h t"), sg_[:, 0:8, :], [sg_], [])
                    K.dma("sp", dstT[:, 128:192, tok:tok + 128].rearrange("h d t -> d h t"), sg_[0:64, 8:16, :], [sg_], [])

            def na_A(g0):
                for grp in pairs:
                    outs = [(Obs[i], Obs[i][:, 0:1024].rearrange("p (h d) -> p h d", h=8)) for i in grp]
                    rms_multi(grp, 8, 128, gains[:, g0:g0 + 128].unsqueeze(1).to_broadcast([128, 8, 128]), outs)

            def na_B(dstT):
                for i, t in enumerate(tiles):
                    tok = t * 128
                    b2 = K.bank()
                    bv = bf(b2)
                    for hh in range(8):
                        K.tr(b2, bv[:, hh * 128:(hh + 1) * 128], Obs[i], Obs[i][:, hh * 128:(hh + 1) * 128], C.ident)
                    sg_ = stg[cnt.oi % 2]
                    cnt.oi += 1
                    K.v("act", "copy", [b2], [sg_], out=sg_[:, 0:8, :], in_=bv.rearrange("p (h t) -> p h t", h=8))
                    K.dma("sp", dstT[:, :, tok:tok + 128].rearrange("h d t -> d h t"), sg_[:, 0:8, :], [sg_], [])

            gsegs = [("g", 8288 + 512 * i, 512) for i in range(12)]
            easy("kr", 2048, 64)
            seg_to_x(1536, 512, 0)
            for grp in pairs:
                outs = [(Obs[i], Obs[i][:, 0:512].rearrange("p (g d) -> p g d", g=1)) for i in grp]
                rms_multi(grp, 1, 512, gains[:, 384:896].unsqueeze(1), outs)
            easy("z", 2112, 512)
            easy("z", 2624, 512)
            easy(*gsegs[0])
            for i in range(n):
                b2 = K.bank()
                bv = bf(b2)
                for c in range(4):
                    K.tr(b2, bv[:, c * 128:(c + 1) * 128], Obs[i], Obs[i][:, c * 128:(c + 1) * 128], C.ident)
                K.v("act", "copy", [b2], [ckTs[i]], out=ckTs[i][:], in_=bv[:, 0:512].rearrange("p (h t) -> p h t", h=4))
            for i, t in enumerate(tiles):
                ckT = ckTs[i]
                KF = Xs[i][:, :].rearrange("p (h d) -> p h d", h=8)
                for nb2 in range(2):
                    b3 = K.bank()
                    for c in range(4):
                        K.mm(b3, b3[:, :], ckT, ckT[:, c, :], wuk, wuk[:, c, nb2 * 512:(nb2 + 1) * 512], c == 0, c == 3)
                    K.v("act", "copy", [b3], [Xs[i]], out=KF[:, 4 * nb2:4 * nb2 + 4, 0:128],
                        in_=b3[:, :].rearrange("p (h d) -> p h d", h=4))
                K.v("dve", "tensor_copy", [kr[i]], [Xs[i]], out=KF[:, :, 128:192],
                    in_=kr[i][:].unsqueeze(1).to_broadcast([128, 8, 64]))
                o = o16[cnt.oi % 2]
                cnt.oi += 1
                for nb2 in range(2):
                    b3 = K.bank()
                    for c in range(4):
                        K.mm(b3, b3[:, :], ckT, ckT[:, c, :], wuv, wuv[:, c, nb2 * 512:(nb2 + 1) * 512], c == 0, c == 3)
                    K.v("act", "copy", [b3], [o], out=o[:, nb2 * 512:(nb2 + 1) * 512], in_=b3[:, :])
                K.dma("sp", C.vm[t * 128:(t + 1) * 128, :], o[:], [o], [])
            heads_A(192)
            easy("dt", 5184, 32)
            easy("nv", 7264, 512)
            easy("nv", 7776, 512)
            easy(*gsegs[1])
            easy(*gsegs[2])
            heads_B(C.kT)
            for qi in range(4):
                seg_to_x(384 * qi, 384, 384 * qi)
            heads_A(0)
            for gsg in gsegs[3:7]:
                easy(*gsg)
            heads_B(C.qT)
            seg_to_x(5216, 512, 0)
            seg_to_x(5728, 512, 512)
            na_A(896)
            for gsg in gsegs[7:10]:
                easy(*gsg)
            na_B(C.nqT)
            seg_to_x(6240, 512, 0)
            seg_to_x(6752, 512, 512)
            na_A(1024)
            for gsg in gsegs[10:12]:
                easy(*gsg)
            easy("xbc", 3136, 512)
            easy("xbc", 3648, 512)
            na_B(C.nkT)
            easy("xbc", 4160, 512)
            easy("xbc", 4672, 512)
            K.S.flush()


def attn_full(K, C, qn, qr, kn, kr_, V, q0, nq, kbs, scale, dstT, W):
    pO = K.pb[W.acc]
    pD = K.pb[W.acc + 1]
    W.acc = 4 + (W.acc - 4 + 2) % 4
    n = len(kbs)

    def qk(ki):
        kb = kbs[ki]
        pS = K.bank()
        K.mm(pS, pS[:, 0:nq], kn, kn[:, kb * 128:(kb + 1) * 128], qn, qn[:, q0:q0 + nq], True, qr is None)
        if qr is not None:
            K.mm(pS, pS[:, 0:nq], kr_, kr_[0:64, kb * 128:(kb + 1) * 128], qr, qr[0:64, q0:q0 + nq], False, True)
        P = W.P[W.pi % len(W.P)]
        W.pi += 1
        K.v("act", "activation", [pS], [P], out=P[:, 0:nq], in_=pS[:, 0:nq], func=AF.Exp, scale=scale)
        return P
    Pn = qk(0)
    for ki, kb in enumerate(kbs):
        P = Pn
        if ki + 1 < n:
            Pn = qk(ki + 1)
        K.mm(pO, pO[:, 0:nq], V, V[:, kb, :], P, P[:, 0:nq], ki == 0, ki == n - 1)
        K.mm(pD, pD[:, 0:nq], C.ones, C.ones[:], P, P[:, 0:nq], ki == 0, ki == n - 1)
    rd = W.rd[W.pi % 2]
    ob = W.ob[W.pi % 2]
    K.v("dve", "reciprocal", [pD], [rd], out=rd[:, 0:nq], in_=pD[:, 0:nq])
    K.v("dve", "tensor_tensor", [pO, rd], [ob], out=ob[:, 0:nq], in0=pO[:, 0:nq], in1=rd[:, 0:nq], op=ALU.mult)
    K.dma("sp", dstT[:, q0:q0 + nq], ob[:, 0:nq], [ob], [])


def phase_mla(K, C, l, need_ctx):
    with K.phase():
        W = NS()
        W.P = K.ring(4, [128, 512], BF16, "P")
        W.rd = K.ring(2, [128, 512], F32, "rd")
        W.ob = K.ring(2, [128, 512], BF16, "ob")
        W.pi = 0
        W.acc = 4
        K.rot = [0, 1, 2, 3]
        qn = K.ring(2, [128, NTOK], BF16, "qn")
        qr = K.ring(2, [64, NTOK], BF16, "qr")
        kn = K.ring(2, [128, NTOK], BF16, "kn")
        kr_ = K.ring(2, [64, NTOK], BF16, "kr")
        Vt = K.ring(2, [128, 18, 128], BF16, "V")
        scale = 192.0 ** -0.5
        for h in range(8):
            a, b, c, d, V = qn[h % 2], qr[h % 2], kn[h % 2], kr_[h % 2], Vt[h % 2]
            K.dma("sp", a[:], C.qT[h, 0:128, :], [], [a])
            K.dma("sp", b[:], C.qT[h, 128:192, :], [], [b])
            K.dma("sp", c[:], C.kT[h, 0:128, :], [], [c])
            K.dma("sp", d[:], C.kT[h, 128:192, :], [], [d])
            for vv in range(3):
                K.dma("sp", V[:, vv * 6:(vv + 1) * 6, :],
                      C.vm[vv * 768:(vv + 1) * 768, h * 128:(h + 1) * 128].rearrange("(kb p) d -> p kb d", p=128), [], [V])
            dstT = C.brT[0, h * 128:(h + 1) * 128, :]
            if need_ctx:
                attn_full(K, C, a, b, c, d, V, 0, 256, [0, 1], scale, dstT, W)
            for qb in range(4):
                attn_full(K, C, a, b, c, d, V, 256 + qb * 512, 512, list(range(18)), scale, dstT, W)


def na_band(i):
    kb10 = min(max(2 * i - 4, 0), 22)
    var = {0: 0, 1: 1, 14: 3, 15: 4}.get(i, 2)
    return kb10, var


def phase_na(K, C, l, need_ctx, side_factory=None):
    with K.phase():
        side = side_factory() if side_factory is not None else None
        W = NS()
        W.P = K.ring(3, [128, 512], BF16, "P")
        W.rd = K.ring(2, [128, 512], F32, "rd")
        W.ob = K.ring(2, [128, 512], BF16, "ob")
        W.pi = 0
        W.acc = 4
        K.rot = [0, 1, 2, 3]
        qn = K.ring(2, [128, NTOK], BF16, "nq")
        kn = K.ring(2, [128, NTOK], BF16, "nk")
        Vt = K.ring(2, [128, 18, 128], BF16, "nV")
        bt = K.ring(2, [128, 25, 128], F32, "nab")
        La = K.ring(2, [128, 640], F32, "La")
        Pn = K.ring(3, [128, 7, 128], BF16, "Pn")
        o4 = K.ring(2, [128, 512], BF16, "o4")
        rdn = K.ring(2, [128, 128], F32, "rdn")
        scale = 128.0 ** -0.5
        li = 0
        for h in range(8):
            a, c, V, bia = qn[h % 2], kn[h % 2], Vt[h % 2], bt[h % 2]
            K.dma("sp", a[:], C.nqT[h, :, :], [], [a])
            K.dma("sp", c[:], C.nkT[h, :, :], [], [c])
            for vv in range(3):
                K.dma("sp", V[:, vv * 6:(vv + 1) * 6, :],
                      C.nv[vv * 768:(vv + 1) * 768, h * 128:(h + 1) * 128].rearrange("(kb p) d -> p kb d", p=128), [], [V])
            for vv in range(5):
                K.dma("sp", bia[:, vv * 5:(vv + 1) * 5, :], C.nab[l, h, vv * 5:(vv + 1) * 5].rearrange("v k q -> k v q"), [], [bia])
            dstT = C.brT[2, h * 128:(h + 1) * 128, :]
            if need_ctx:
                attn_full(K, C, a, None, c, None, V, 0, 256, [0, 1], scale, dstT, W)
            def st1(i):
                kb10, var = na_band(i)
                qtok = 256 + i * 128
                kblk0 = 2 + kb10 // 2
                pA = K.bank()
                pB = K.bank()
                for j in range(4):
                    K.mm(pA, pA[:, j * 128:(j + 1) * 128], c, c[:, (kblk0 + j) * 128:(kblk0 + j + 1) * 128], a, a[:, qtok:qtok + 128], True, True)
                K.mm(pB, pB[:, 0:128], c, c[:, (kblk0 + 4) * 128:(kblk0 + 5) * 128], a, a[:, qtok:qtok + 128], True, True)
                for cc in range(2):
                    K.mm(pB, pB[:, 128 + cc * 128:256 + cc * 128], c, c[:, cc * 128:(cc + 1) * 128], a, a[:, qtok:qtok + 128], True, True)
                la = La[st1.n % 2]
                P = Pn[st1.n % 3]
                st1.n += 1
                bv = bia[:, var * 5:var * 5 + 5, :]
                K.v("dve", "scalar_tensor_tensor", [pA, bia], [la], out=la[:, 0:512], in0=pA[:, :], scalar=scale,
                    in1=bv[:, 0:4, :].rearrange("p j q -> p (j q)"), op0=ALU.mult, op1=ALU.add)
                K.v("dve", "scalar_tensor_tensor", [pB, bia], [la], out=la[:, 512:640], in0=pB[:, 0:128], scalar=scale,
                    in1=bv[:, 4, :], op0=ALU.mult, op1=ALU.add)
                Pf = P[:].rearrange("p j q -> p (j q)")
                K.v("act", "activation", [la], [P], out=Pf[:, 0:640], in_=la[:, :], func=AF.Exp)
                K.v("act", "activation", [pB], [P], out=Pf[:, 640:896], in_=pB[:, 128:384], func=AF.Exp, scale=scale)
                return P
            st1.n = li

            def st2(i, P):
                kb10, var = na_band(i)
                qtok = 256 + i * 128
                kblk0 = 2 + kb10 // 2
                pO = K.pb[W.acc]
                pD = K.pb[W.acc + 1]
                W.acc = 4 + (W.acc - 4 + 2) % 4
                for j in range(7):
                    vb = kblk0 + j if j < 5 else j - 5
                    K.mm(pO, pO[:, 0:128], V, V[:, vb, :], P, P[:, j, :], j == 0, j == 6)
                    K.mm(pD, pD[:, 0:128], C.ones, C.ones[:], P, P[:, j, :], j == 0, j == 6)
                rd = rdn[i % 2]
                ob = o4[(i // 4) % 2]
                K.v("dve", "reciprocal", [pD], [rd], out=rd[:], in_=pD[:, 0:128])
                K.v("dve", "tensor_tensor", [pO, rd], [ob], out=ob[:, (i % 4) * 128:(i % 4 + 1) * 128], in0=pO[:, 0:128], in1=rd[:], op=ALU.mult)
                if i % 4 == 3:
                    K.dma("sp", dstT[:, qtok - 384:qtok + 128], ob[:], [ob], [])
            Pnext = st1(0)
            for i in range(16):
                Pc = Pnext
                if i + 1 < 16:
                    Pnext = st1(i + 1)
                st2(i, Pc)
                if side is not None and i % 4 == 1:
                    next(side, None)
            li = st1.n


def phase_merge(K, C, l, blocks, src, dst):
    with K.phase():
        K.rot = list(range(8))
        Gr = [K.sb([128, D], F32, "G") for _ in range(2)]
        bT = [K.sb([128, 8, 768], BF16, "bT") for _ in range(3)]
        wbr = [K.ring(2, [128, 8, 512], BF16, "wbr") for _ in range(3)]
        gt = K.ring(3, [128, 512], F32, "gt")
        mb = [K.sb([128, D], BF16, "mb") for _ in range(6)]
        mT = K.sb([128, 16, 768], BF16, "mT")
        wo = K.ring(2, [128, 16, 512], BF16, "wo")
        hs = K.ring(2, [128, 512], F32, "hs")
        ho = K.ring(2, [128, 512], F32, "ho")
        acc = K.ring(3, [128, 512], F32, "acc")
        tmpg = K.ring(2, [128, 512], F32, "tg")
        wi = 0
        gi = 0
        ai = 0
        for tiles in blocks:
            n = len(tiles)
            ntok = n * 128
            tok0 = tiles[0] * 128
            for r in sorted(set(1 if t < 2 else 0 for t in tiles)):
                load_mod(K, C, l, r, 5, Gr[r])
            for bi in range(3):
                K.dma("sp", bT[bi][:, :, 0:ntok], C.brT[bi].rearrange("(kc p) t -> p kc t", p=128)[:, :, tok0:tok0 + ntok], [], [bT[bi]])
            for nb in range(4):
                ws = []
                for bi in range(3):
                    w = wbr[bi][wi % 2]
                    K.dma("pool", w[:], C.w_branch[l, bi].rearrange("(kc p) n -> p kc n", p=128)[:, :, nb * 512:(nb + 1) * 512], [], [w])
                    ws.append(w)
                wi += 1
                for i, t in enumerate(tiles):
                    ac = acc[ai % 3]
                    ai += 1
                    for bi in range(3):
                        g = gt[gi % 3]
                        K.dma("act", g[:], C.gts[t * 128:(t + 1) * 128, bi * D + nb * 512:bi * D + (nb + 1) * 512], [], [g])
                        bk = K.bank()
                        for kc in range(8):
                            K.mm(bk, bk[:, :], bT[bi], bT[bi][:, kc, i * 128:(i + 1) * 128], ws[bi], ws[bi][:, kc, :], kc == 0, kc == 7)
                        mslice = mb[i][:, nb * 512:(nb + 1) * 512]
                        if bi == 0:
                            K.v("dve", "tensor_tensor", [bk, g], [ac], out=ac[:], in0=bk[:, :], in1=g[:], op=ALU.mult)
                        else:
                            tg = tmpg[gi % 2]
                            K.v("dve", "tensor_tensor", [bk, g], [tg], out=tg[:], in0=bk[:, :], in1=g[:], op=ALU.mult)
                            if bi == 1:
                                K.v("pool", "tensor_tensor", [tg, ac], [ac], out=ac[:], in0=ac[:], in1=tg[:], op=ALU.add)
                            else:
                                K.v("pool", "tensor_tensor", [tg, ac], [mb[i]], out=mslice, in0=ac[:], in1=tg[:], op=ALU.add)
                        gi += 1
            for i in range(n):
                m2 = mb[i]
                for g2 in range(2):
                    bk = K.bank()
                    bv = bf(bk)
                    for c in range(8):
                        cc = g2 * 8 + c
                        K.tr(bk, bv[:, c * 128:(c + 1) * 128], m2, m2[:, cc * 128:(cc + 1) * 128], C.ident)
                    K.v("act" if g2 == 0 else "dve", "copy" if g2 == 0 else "tensor_copy", [bk], [mT],
                        out=mT[:, g2 * 8:(g2 + 1) * 8, i * 128:(i + 1) * 128], in_=bv.rearrange("p (c t) -> p c t", c=8))
            for nb in range(4):
                w = wo[nb % 2]
                K.dma("pool", w[:], C.w_out[l].rearrange("(kc p) n -> p kc n", p=128)[:, :, nb * 512:(nb + 1) * 512], [], [w])
                for i, t in enumerate(tiles):
                    bk = K.bank()
                    for kc in range(16):
                        K.mm(bk, bk[:, :], mT, mT[:, kc, i * 128:(i + 1) * 128], w, w[:, kc, :], kc == 0, kc == 15)
                    h1 = hs[(nb * n + i) % 2]
                    o1 = ho[(nb * n + i) % 2]
                    G = Gr[1 if t < 2 else 0]
                    K.dma("sp", h1[:], src(t)[:, nb * 512:(nb + 1) * 512], [], [h1])
                    K.v("dve", "tensor_tensor", [bk, G], [o1], out=o1[:], in0=bk[:, :], in1=G[:, nb * 512:(nb + 1) * 512], op=ALU.mult)
                    K.v("dve", "tensor_tensor", [o1, h1], [o1], out=o1[:], in0=o1[:], in1=h1[:], op=ALU.add)
                    K.dma("sp", dst(t)[:, nb * 512:(nb + 1) * 512], o1[:], [o1], [])


def phase_conv(K, C, l, side_factory=None):
    with K.phase():
        side = side_factory() if side_factory is not None else None
        cw = K.sb([128, 16, 5], F32, "cw")
        cb = K.sb([128, 16], F32, "cb")
        K.dma("sp", cw[:], C.convw[l], [], [cw])
        K.dma("sp", cb[:], C.convb[l], [], [cb])
        xp = K.ring(2, [128, 260 + 2052], F32, "xp")
        yy = K.ring(2, [128, NTOK], F32, "yy")
        yo = K.ring(2, [128, NTOK], F32, "yo")
        for b in xp:
            K.v("pool", "memset", [], [b], ap=b[:], constant=0.0)
        for cc in range(16):
            x = xp[cc % 2]
            y = yy[cc % 2]
            o = yo[cc % 2]
            K.dma("sp", x[:, 2:258], C.xbcT[cc * 128:(cc + 1) * 128, 0:256], [], [x])
            K.dma("sp", x[:, 262:262 + 2048], C.xbcT[cc * 128:(cc + 1) * 128, 256:NTOK], [], [x])
            for (eng, x0, y0, n) in (("dve", 0, 0, 256), ("dve", 260, 256, 2048)):
                K.v(eng, "tensor_scalar", [x, cw, cb], [y], out=y[:, y0:y0 + n], in0=x[:, x0:x0 + n], scalar1=cw[:, cc, 0:1],
                    scalar2=cb[:, cc:cc + 1], op0=ALU.mult, op1=ALU.add)
                for j in range(1, 5):
                    K.v(eng, "scalar_tensor_tensor", [x, cw, y], [y], out=y[:, y0:y0 + n], in0=x[:, x0 + j:x0 + j + n],
                        scalar=cw[:, cc, j:j + 1], in1=y[:, y0:y0 + n], op0=ALU.mult, op1=ALU.add)
            K.v("act", "activation", [y], [o], out=o[:], in_=y[:], func=AF.Silu)
            K.dma("sp", C.xbc2T[cc * 128:(cc + 1) * 128, :], o[:], [o], [])
            if side is not None:
                for _ in range(3):
                    next(side, None)
        if side is not None:
            for _ in side:
                pass


def phase_ssd(K, C, l, need_ctx):
    with K.phase():
        K.rot = [0, 1, 2, 3, 4, 5]
        W = NS()
        W.sq = K.sb([128, 1536], F32, "sq")
        W.gst = K.sb([128, 24], F32, "gst")
        gn = K.sb([128, 1024], F32, "gn")
        W.gains = gn
        K.dma("sp", gn[:], bcast_rows(C.ssd_norm_g[l, :], 128), [], [gn])
        an = K.sb([128, 32], F32, "an")
        K.dma("sp", an[:], bcast_rows(C.ssd_a_log[l].rearrange("a h -> (a h)"), 128), [], [an])
        K.v("act", "activation", [an], [an], out=an[:], in_=an[:], func=AF.Exp)
        K.v("dve", "tensor_scalar", [an], [an], out=an[:], in0=an[:], scalar1=-1.0, scalar2=None, op0=ALU.mult)
        dsk = K.sb([128, 16], F32, "dsk")
        K.dma("sp", dsk[:], bcast_rows(C.ssd_d[l, :], 128), [], [dsk])
        msk = K.sb([128, 8, 128], F32, "msk")
        K.dma("sp", msk[:], C.cst[:, :, :], [], [msk])
        U, Lm, SL, SU = msk[:, 1, :], msk[:, 2, :], msk[:, 3, :], msk[:, 4, :]
        Sb_all = K.sb([128, 18, 1024], BF16, "Sball")
        Sf = K.sb([128, 1024], F32, "Sf")
        Sb = K.sb([128, 1024], F32, "Sb")
        Sfb = K.sb([128, 1024], BF16, "Sfb")
        xin = K.ring(2, [128, 16, 128], F32, "xin")
        dtc = K.ring(2, [128, 32], F32, "dtc")
        dta = K.ring(2, [128, 32], F32, "dta")
        xs = K.ring(2, [128, 1024], F32, "xs")
        Btm = K.ring(2, [128, 4, 128], BF16, "Btm")
        BT = K.ring(2, [128, 4, 128], BF16, "BT")
        CT = K.ring(2, [128, 4, 128], BF16, "CT")
        Yd = K.ring(2, [128, 16, 128], F32, "Yd")
        Md = K.ring(2, [128, 16, 128], BF16, "Md")
        CEd = K.ring(2, [128, 16, 128], BF16, "CEd")
        tD = K.ring(2, [128, 512], F32, "tD")
        tE = K.ring(2, [128, 512], F32, "tE")
        Gm = K.ring(2, [128, 4, 128], F32, "Gm")
        xdt = K.ring(2, [128, 16, 64], BF16, "xdt")
        wx = K.ring(2, [128, 16, 64], BF16, "wx")
        sm = K.ring(4, [128, 64], F32, "sm")
        zt = K.ring(2, [128, 1024], F32, "zt")
        v1 = K.ring(2, [128, 1024], F32, "v1")
        ob = K.ring(2, [128, 1024], BF16, "ob")
        stg = K.ring(2, [128, 8, 128], BF16, "stg")
        xv = C.xbc2T.rearrange("(cc p) t -> p cc t", p=128)
        cnt = [0]

        def prep(c, ncc):
            i = cnt[0]
            cnt[0] += 1
            x = xin[i % 2]
            K.dma("sp", x[:, 0:ncc, :], xv[:, 0:ncc, c * 128:(c + 1) * 128], [], [x])
            d = dtc[i % 2]
            K.dma("sp", d[:], C.dts[c * 128:(c + 1) * 128, :], [], [d])
            da = dta[i % 2]
            K.v("dve", "tensor_tensor", [d, an], [da], out=da[:], in0=d[:], in1=an[:], op=ALU.mult)
            xt = xs[i % 2]
            for g in range(2):
                bk = K.bank()
                for cc in range(4):
                    K.tr(bk, bk[:, cc * 128:(cc + 1) * 128], x, x[:, g * 4 + cc, :], C.identf)
                K.v("act", "copy", [bk], [xt], out=xt[:, g * 512:(g + 1) * 512], in_=bk[:, :])
            bt = Btm[i % 2]
            bk = K.bank()
            for g in range(4):
                K.tr(bk, bk[:, g * 128:(g + 1) * 128], x, x[:, 8 + g, :], C.identf)
            K.v("act", "copy", [bk], [bt], out=bt[:], in_=bk[:, :].rearrange("p (g n) -> p g n", g=4))
            return i, x, d, da, xt, bt

        def small_exp(lhsT_t, lhsT, da, d0):
            t = sm[cnt[0] % 4]
            cnt[0] += 1
            bk = K.bank()
            K.mm(bk, bk[:, 0:16], lhsT_t, lhsT, da, da[:, d0:d0 + 16], True, True)
            K.v("act", "activation", [bk], [t], out=t[:, 0:16], in_=bk[:, 0:16], func=AF.Exp)
            return t

        def state_update(St, Stb, bt, xt, d, da, d0, wmask_t, wmask, i):
            wv = small_exp(wmask_t, wmask, da, d0)
            dec = small_exp(C.onesf, C.onesf[:], da, d0)
            K.v("dve", "tensor_tensor", [wv, d], [wv], out=wv[:, 16:32], in0=wv[:, 0:16], in1=d[:, d0:d0 + 16], op=ALU.mult)
            w_ = wx[i % 2]
            K.v("dve", "tensor_tensor", [xt, wv], [w_], out=w_[:], in0=xt[:].rearrange("p (h q) -> p h q", h=16),
                in1=wv[:, 16:32].unsqueeze(2).to_broadcast([128, 16, 64]), op=ALU.mult)
            K.v("dve", "tensor_tensor", [St, dec], [St], out=St[:].rearrange("p (h q) -> p h q", h=16),
                in0=St[:].rearrange("p (h q) -> p h q", h=16), in1=dec[:, 0:16].unsqueeze(2).to_broadcast([128, 16, 64]), op=ALU.mult)
            for hf in range(2):
                bk = K.bank()
                for gg in range(2):
                    g = hf * 2 + gg
                    K.mm(bk, bk[:, gg * 256:(gg + 1) * 256], bt, bt[:, g, :], w_, w_[:, 4 * g:4 * g + 4, :].rearrange("p h q -> p (h q)"), True, True)
                K.v("dve", "tensor_tensor", [St, bk], [St], out=St[:, hf * 512:(hf + 1) * 512], in0=St[:, hf * 512:(hf + 1) * 512],
                    in1=bk[:, :], op=ALU.add)
            if Stb is not None:
                K.v("act", "copy", [St], [Stb], out=Stb[:], in_=St[:])

        K.v("pool", "memset", [], [Sb], ap=Sb[:], constant=0.0)
        K.v("pool", "memset", [], [Sf], ap=Sf[:], constant=0.0)
        K.v("pool", "memset", [], [Sfb], ap=Sfb[:], constant=0.0)
        for c in [1, 0] + list(range(17, 1, -1)):
            i, x, d, da, xt, bt = prep(c, 12)
            K.v("act", "copy", [Sb], [Sb_all], out=Sb_all[:, c, :], in_=Sb[:])
            state_update(Sb, None, bt, xt, d, da, 16, msk, SU, i)
        for c in range(18):
            i, x, d, da, xt, bt = prep(c, 16)
            emit_y = need_ctx or c >= 2
            if emit_y:
                z = zt[i % 2]
                K.dma("sp", z[:], C.zs[c * 128:(c + 1) * 128, :], [], [z])
                bT_, cT_ = BT[i % 2], CT[i % 2]
                K.v("act", "copy", [x], [bT_], out=bT_[:], in_=x[:, 8:12, :])
                K.v("act", "copy", [x], [cT_], out=cT_[:], in_=x[:, 12:16, :])
                bG = K.bank()
                for g in range(4):
                    K.mm(bG, bG[:, g * 128:(g + 1) * 128], bT_, bT_[:, g, :], cT_, cT_[:, g, :], True, True)
                for di in range(2):
                    K.v("dve", "tensor_tensor", [bG, msk], [Gm[di]], out=Gm[di][:], in0=bG[:, :].rearrange("p (g n) -> p g n", g=4),
                        in1=(U if di == 0 else Lm).unsqueeze(1).to_broadcast([128, 4, 128]), op=ALU.mult)
                Ms, CEs, xds = [], [], []
                for di in range(2):
                    d0 = 16 * di
                    mk = U if di == 0 else Lm
                    sl = SL if di == 0 else SU
                    gm = Gm[di]
                    Y = Yd[di]
                    K.v("pool", "tensor_tensor", [msk, da], [Y], out=Y[:], in0=mk.unsqueeze(1).to_broadcast([128, 16, 128]),
                        in1=da[:, d0:d0 + 16].unsqueeze(2).to_broadcast([128, 16, 128]), op=ALU.mult)
                    M = Md[di]
                    CE = CEd[di]
                    for hg in range(4):
                        yv = Y[:, 4 * hg:4 * hg + 4, :].rearrange("p h n -> p (h n)")
                        b1 = K.bank()
                        K.mm(b1, b1[:, :], msk, sl, Y, yv, True, True)
                        t1 = tD[hg % 2]
                        K.v("act", "activation", [b1], [t1], out=t1[:], in_=b1[:, :], func=AF.Exp)
                        K.v("dve", "tensor_tensor", [t1, gm], [M], out=M[:, 4 * hg:4 * hg + 4, :], in0=t1[:].rearrange("p (h n) -> p h n", h=4),
                            in1=gm[:, hg, :].unsqueeze(1).to_broadcast([128, 4, 128]), op=ALU.mult)
                        b2 = K.bank()
                        K.mm(b2, b2[:, :], C.onesf, C.onesf[:], Y, yv, True, True)
                        t2 = tE[hg % 2]
                        K.v("act", "activation", [b2], [t2], out=t2[:], in_=b2[:, :], func=AF.Exp)
                        K.v("pool", "tensor_tensor", [t2, cT_], [CE], out=CE[:, 4 * hg:4 * hg + 4, :], in0=t2[:].rearrange("p (h n) -> p h n", h=4),
                            in1=cT_[:, hg, :].unsqueeze(1).to_broadcast([128, 4, 128]), op=ALU.mult)
                    xd = xdt[di]
                    K.v("dve", "tensor_tensor", [xt, d], [xd], out=xd[:], in0=xt[:].rearrange("p (h q) -> p h q", h=16),
                        in1=d[:, d0:d0 + 16].unsqueeze(2).to_broadcast([128, 16, 64]), op=ALU.mult)
                    Ms.append(M)
                    CEs.append(CE)
                    xds.append(xd)
                yb = [K.pb[6], K.pb[7]]
                for h in range(16):
                    bk = yb[h // 8]
                    o = bk[:, (h % 8) * 64:(h % 8 + 1) * 64]
                    K.mm(bk, o, Ms[0], Ms[0][:, h, :], xds[0], xds[0][:, h, :], True, False)
                    K.mm(bk, o, Ms[1], Ms[1][:, h, :], xds[1], xds[1][:, h, :], False, False)
                    K.mm(bk, o, CEs[0], CEs[0][:, h, :], Sfb, Sfb[:, h * 64:(h + 1) * 64], False, False)
                    K.mm(bk, o, CEs[1], CEs[1][:, h, :], Sb_all, Sb_all[:, c, h * 64:(h + 1) * 64], False, True)
                v = v1[i % 2]
                K.v("dve", "tensor_tensor", [xt, dsk], [v], out=v[:].rearrange("p (h q) -> p h q", h=16),
                    in0=xt[:].rearrange("p (h q) -> p h q", h=16), in1=dsk[:].unsqueeze(2).to_broadcast([128, 16, 64]), op=ALU.mult)
                for hf in range(2):
                    K.v("dve", "tensor_tensor", [v, yb[hf]], [v], out=v[:, hf * 512:(hf + 1) * 512], in0=v[:, hf * 512:(hf + 1) * 512],
                        in1=yb[hf][:, :], op=ALU.add)
                if C.dbgv is not None:
                    K.dma("sp", C.dbgv[c * 128:(c + 1) * 128, :], v[:], [v], [])
                K.v("pool", "tensor_tensor", [v, z], [v], out=v[:], in0=v[:], in1=z[:], op=ALU.mult)
                o_ = ob[i % 2]
                rms_groups(K, v, v[:].rearrange("p (g q) -> p g q", g=4), 4, 256, gn[:].rearrange("p (g q) -> p g q", g=4),
                           o_, o_[:].rearrange("p (g q) -> p g q", g=4), W)
                bk = K.bank()
                bv = bf(bk)
                for cc in range(8):
                    K.tr(bk, bv[:, cc * 128:(cc + 1) * 128], o_, o_[:, cc * 128:(cc + 1) * 128], C.ident)
                sg_ = stg[i % 2]
                K.v("act", "copy", [bk], [sg_], out=sg_[:], in_=bv.rearrange("p (c t) -> p c t", c=8))
                K.dma("sp", C.brT[1].rearrange("(kc p) t -> p kc t", p=128)[:, :, c * 128:(c + 1) * 128], sg_[:], [sg_], [])
            if c < 17:
                state_update(Sf, Sfb, bt, xt, d, da, 0, msk, SL, i)
        if C.dbgv is not None:
            K.dma("sp", C.dbgS[:, :], Sb_all[:].rearrange("p c f -> p (c f)"), [Sb_all], [])

def tile_rows(ap2d, t):
    return ap2d[t * 128:(t + 1) * 128, :]


def build(dbg=(), stop_after=None):
    nc = bass.Bass("TRN2", target_bir_lowering=False)
    C = NS()

    def inp(name, shape, dt=F32):
        return nc.dram_tensor(name, list(shape), dt, kind="ExternalInput").ap()

    C.x = inp("x", [T, D])
    C.ctx = inp("ctx", [TC, D])
    C.cT = inp("cT", [128, 16, 2])
    C.cst = inp("cst", [128, 8, 128])
    C.w_ada = inp("w_ada", [L, D, 9 * D])
    C.b_ada = inp("b_ada", [L, 9 * D])
    C.norm_g = inp("norm_g", [L, 3, D])
    C.ffn_w = {}
    for nm in ("ffn1", "ffn2"):
        C.ffn_w[nm] = (inp(nm + "_w_gate", [L, D, DFF]), inp(nm + "_w_up", [L, D, DFF]), inp(nm + "_w_down", [L, DFF, D]))
    C.w_in = inp("w_in", [L, D, DIN])
    C.mla_kv_norm_g = inp("mla_kv_norm_g", [L, 512])
    C.mla_w_uk = inp("mla_w_uk", [L, 512, 1024])
    C.mla_w_uv = inp("mla_w_uv", [L, 512, 1024])
    C.mla_q_norm_g = inp("mla_q_norm_g", [L, 192])
    C.mla_k_norm_g = inp("mla_k_norm_g", [L, 192])
    C.ssd_dt_bias = inp("ssd_dt_bias", [L, 2, 16])
    C.na_q_norm_g = inp("na_q_norm_g", [L, 128])
    C.na_k_norm_g = inp("na_k_norm_g", [L, 128])
    C.rope = inp("rope", [T, 64])
    C.nab = inp("nab", [L, 8, 25, 128, 128])
    C.convw = inp("convw", [L, 128, 16, 5])
    C.convb = inp("convb", [L, 128, 16])
    C.ssd_a_log = inp("ssd_a_log", [L, 2, 16])
    C.ssd_d = inp("ssd_d", [L, 16])
    C.ssd_norm_g = inp("ssd_norm_g", [L, 1024])
    C.w_branch = inp("w_branch", [L, 3, 1024, D])
    C.w_out = inp("w_out", [L, D, D])
    out = nc.dram_tensor("out", [T, D], F32, kind="ExternalOutput").ap()

    with ExitStack() as es:
        K = KB(nc, es, dbg)
        C.modr = [K.dram("modr%d" % l, [2, 9 * D], F32) for l in range(L)]
        C.modraw = [K.dram("modraw%d" % l, [2, 9 * D], F32) for l in range(L)]
        hA = K.dram("hA", [NTOK, D], F32)
        hB = K.dram("hB", [NTOK, D], F32)
        C.qT = K.dram("qT", [8, 192, NTOK], BF16)
        C.kT = K.dram("kT", [8, 192, NTOK], BF16)
        C.vm = K.dram("vm", [NTOK, 1024], BF16)
        C.nqT = K.dram("nqT", [8, 128, NTOK], BF16)
        C.nkT = K.dram("nkT", [8, 128, NTOK], BF16)
        C.nv = K.dram("nv", [NTOK, 1024], BF16)
        C.xbcT = K.dram("xbcT", [2048, NTOK], F32)
        C.zs = K.dram("zs", [NTOK, 1024], F32)
        C.xbc2T = K.dram("xbc2T", [2048, NTOK], F32)
        C.dbgv = K.dram("dbgv", [NTOK, 1024], F32) if "dbgv" in K.dbg else None
        C.dbgS = K.dram("dbgS", [128, 18 * 1024], BF16, kind="ExternalOutput") if "dbgv" in K.dbg else None
        C.dts = K.dram("dts", [NTOK, 32], F32)
        C.gts = K.dram("gts", [NTOK, 6144], F32)
        C.brT = K.dram("brT", [3, 1024, NTOK], BF16)

        def done(tag):
            return stop_after == tag

        phase_consts(K, C)
        phase_adaln(K, C, 0)

        def src0(t):
            return tile_rows(C.ctx, t) if t < 2 else tile_rows(C.x, t - 2)

        def rA(t):
            return tile_rows(hA, t)

        def rB(t):
            return tile_rows(hB, t)

        def rOut(t):
            return tile_rows(out, t - 2)
        lat_blocks = [list(range(2, 8)), list(range(8, 14)), list(range(14, 18))]
        blocks_all = [list(range(0, 6)), list(range(6, 12)), list(range(12, 18))]
        for l in range(L):
            need_ctx = l < L - 1
            if l == 0:
                f1s, f1d, mgd, f2d = src0, rA, rB, rA
            else:
                f1s, f1d, mgd, f2d = rA, rB, rA, rOut
            wg, wu, wd = C.ffn_w["ffn1"]
            phase_ffn(K, C, l, 0, wg[l], wu[l], wd[l], blocks_all, f1s, f1d)
            phase_win(K, C, l, blocks_all, f1d)
            phase_mla(K, C, l, need_ctx)
            if l == 0:
                phase_na(K, C, l, need_ctx, lambda: gen_adaln_blocks(K, C, 1, 0, 14))
                phase_conv(K, C, l, lambda: gen_adaln_blocks(K, C, 1, 14, 36))
                phase_adaln_finish(K, C, 1)
            else:
                phase_na(K, C, l, need_ctx)
                phase_conv(K, C, l)
            phase_ssd(K, C, l, need_ctx)
            blk2 = blocks_all if need_ctx else lat_blocks
            phase_merge(K, C, l, blk2, f1d, mgd)
            wg, wu, wd = C.ffn_w["ffn2"]
            phase_ffn(K, C, l, 2, wg[l], wu[l], wd[l], blk2, mgd, f2d)
            if stop_after == "l0" and l == 0:
                break
        K.S.finish()
        print("total ops", K.S.total)
    return nc


def host_consts():
    cst = np.zeros((128, 8, 128), np.float32)
    cst[:, 0, :] = np.eye(128, dtype=np.float32)
    k = np.arange(128)[:, None]
    j = np.arange(128)[None, :]
    cst[:, 1, :] = (k <= j)
    cst[:, 2, :] = (k >= j)
    cst[:, 3, :] = (k > j)
    cst[:, 4, :] = (k < j)
    return cst


def na_bias_layout(rpb):
    Ln = rpb.shape[0]
    out = np.full((Ln, 8, 5, 5, 128, 128), NEG, np.float32)
    rep = [0, 1, 5, 14, 15]
    kk = np.arange(640)
    krl = kk // 64
    kc = kk % 64
    ql = np.arange(128)
    qrl = ql // 64
    qc = ql % 64
    for v, i in enumerate(rep):
        kb10 = min(max(2 * i - 4, 0), 22)
        krow = kb10 + krl
        r = 2 * i + qrl
        ws = np.clip(r - 4, 0, 24)
        wc = np.clip(qc - 8, 0, 48)
        okr = (krow[:, None] >= ws[None, :]) & (krow[:, None] < ws[None, :] + 8)
        okc = (kc[:, None] >= wc[None, :]) & (kc[:, None] < wc[None, :] + 16)
        ok = okr & okc
        ro = np.clip(krow[:, None] - r[None, :] + 7, 0, 14)
        co = np.clip(kc[:, None] - qc[None, :] + 15, 0, 30)
        g = rpb[:, :, ro, co]
        g = np.where(ok[None, None], g, np.float32(NEG)).astype(np.float32)
        out[:, :, v] = g.reshape(Ln, 8, 5, 128, 128)
    return np.ascontiguousarray(out.reshape(Ln, 8, 25, 128, 128))


def rope_table():
    t = np.arange(T)
    pos = np.stack([t // 64, t % 64], axis=-1).astype(np.float32)
    inv = (10000.0 ** (-np.arange(16, dtype=np.float32) / 16)).astype(np.float32)
    ang = pos[:, :, None] * inv
    return np.concatenate([np.cos(ang).reshape(T, 32), np.sin(ang).reshape(T, 32)], axis=1).astype(np.float32)


def make_in_maps(inputs, ncores=8):
    cst = host_consts()
    rope = rope_table()
    nab = na_bias_layout(inputs["na_rpb"])
    maps = []
    for b in range(ncores):
        cv = np.stack([inputs["c"][b], inputs["c_ctx"]], axis=0)
        cT = np.ascontiguousarray(cv.reshape(2, 16, 128).transpose(2, 1, 0))
        m = {"x": np.ascontiguousarray(inputs["x"][b]), "ctx": np.ascontiguousarray(inputs["ctx"][b]),
             "cT": cT, "cst": cst}
        for k in ("w_ada", "b_ada", "norm_g", "ffn1_w_gate", "ffn1_w_up", "ffn1_w_down",
                  "ffn2_w_gate", "ffn2_w_up", "ffn2_w_down", "w_in", "mla_kv_norm_g", "mla_w_uk", "mla_w_uv",
                  "mla_q_norm_g", "mla_k_norm_g", "ssd_dt_bias", "na_q_norm_g", "na_k_norm_g"):
            m[k] = inputs[k]
        m["rope"] = rope
        m["nab"] = nab
        m["convw"] = np.ascontiguousarray(inputs["ssd_conv_w"].reshape(L, 5, 16, 128).transpose(0, 3, 2, 1))
        m["convb"] = np.ascontiguousarray(inputs["ssd_conv_b"].reshape(L, 16, 128).transpose(0, 2, 1))
        for k in ("ssd_a_log", "ssd_d", "ssd_norm_g"):
            m[k] = inputs[k]
        m["w_branch"] = inputs["w_branch"]
        m["w_out"] = inputs["w_out"]
        maps.append(m)
    return maps


def kernel(**inputs):
    inputs = {k: np.asarray(v) for k, v in inputs.items()}
    nc = build()
    maps = make_in_maps(inputs, 8)
    res = run_bass_kernel_spmd(nc, maps, core_ids=list(range(8)))
    return np.stack([r["out"] for r in res.results], axis=0)
```

```python
import numpy as np
from contextlib import ExitStack
import concourse.bass as bass
import concourse.mybir as mybir
from concourse.bass_utils import run_bass_kernel_spmd

F32 = mybir.dt.float32
BF16 = mybir.dt.bfloat16
ALU = mybir.AluOpType
AF = mybir.ActivationFunctionType
AX = mybir.AxisListType

D = 2048
DFF = 5632
T = 2048
TC = 256
NTOK = 2304
NT = 18
L = 2
DIN = 14432
EPS = 1e-6
NEG = -30000.0


class Buf:
    __slots__ = ("name", "lw", "rd", "rd_dma", "excl")

    def __init__(self, name=""):
        self.name = name
        self.lw = None
        self.rd = {}
        self.rd_dma = []
        self.excl = False


class Op:
    __slots__ = ("eng", "fn", "cdeps", "ddeps", "dma", "sem", "semval", "marked", "count", "seq", "gen")


class Tile:
    def __init__(self, t, name=""):
        self.t = t
        self.buf = Buf(name)

    def __getitem__(self, idx):
        return self.t[idx]


DMA_R = {"sp": 8, "act": 8, "pool": 8}


def _buf(b):
    return b if isinstance(b, Buf) else b.buf


class Sched:
    CE = ("pe", "act", "dve", "pool")
    ALLE = ("pe", "act", "dve", "pool", "sp")

    def __init__(self, nc, es):
        self.nc = nc
        self.psem = {e: es.enter_context(nc.semaphore("ps_" + e)) for e in self.CE}
        self.dsem = {q: [es.enter_context(nc.semaphore("ds_%s%d" % (q, i))) for i in range(r)]
                     for q, r in DMA_R.items()}
        self.dcount = {q: 0 for q in DMA_R}
        self.pcount = {e: 0 for e in self.CE}
        self.waited = {e: {} for e in self.ALLE}
        self.ops = []
        self.barrier_c = {}
        self.barrier_d = []
        self.last_dma = {q: [None] * r for q, r in DMA_R.items()}
        self.seq = 0
        self.gen = 0
        self.total = 0

    def op(self, eng, fn, reads=(), writes=(), dma=False):
        o = Op()
        o.eng = eng
        o.fn = fn
        o.dma = dma
        o.marked = False
        o.count = None
        o.sem = None
        o.semval = None
        o.seq = self.seq
        o.gen = self.gen
        self.seq += 1
        cd = dict(self.barrier_c)
        dl = list(self.barrier_d)
        gen = self.gen

        def add(d):
            if d.gen != gen:
                return
            if d.dma:
                dl.append(d)
            else:
                if d.eng == "pe" and eng == "pe" and not dma:
                    return
                cur = cd.get(d.eng)
                if cur is None or cur.seq < d.seq:
                    cd[d.eng] = d

        for b in reads:
            b = _buf(b)
            if b.lw is not None:
                add(b.lw)
            if b.excl:
                for e2, d in b.rd.items():
                    if e2 != eng:
                        add(d)
        for b in writes:
            b = _buf(b)
            if b.lw is not None:
                add(b.lw)
            for d in b.rd.values():
                add(d)
            for d in b.rd_dma:
                add(d)
        if dma:
            q = eng
            i = self.dcount[q]
            self.dcount[q] = i + 1
            r = DMA_R[q]
            slot = i % r
            o.sem = self.dsem[q][slot]
            o.semval = 16 * (i // r + 1)
            prev = self.last_dma[q][slot]
            if prev is not None:
                dl.append(prev)
            self.last_dma[q][slot] = o
        for d in cd.values():
            d.marked = True
        o.cdeps = cd
        o.ddeps = dl
        for b in reads:
            b = _buf(b)
            if dma:
                b.rd_dma.append(o)
            else:
                b.rd[eng] = o
        for b in writes:
            b = _buf(b)
            b.lw = o
            b.rd = {}
            b.rd_dma = []
        self.ops.append(o)
        return o

    def flush(self):
        nc = self.nc
        ops = self.ops
        self.ops = []
        if not ops:
            return
        self.total += len(ops)
        lastc = {}
        for o in ops:
            if (not o.dma) and o.eng in self.psem:
                lastc[o.eng] = o
        for o in lastc.values():
            o.marked = True
        for o in ops:
            if (not o.dma) and o.marked and o.eng in self.psem:
                self.pcount[o.eng] += 1
                o.count = self.pcount[o.eng]
        per = {e: [] for e in self.ALLE}
        for o in ops:
            per[o.eng].append(o)
        psem = self.psem

        def emit(ename, engobj):
            w = self.waited[ename]
            for o in per[ename]:
                for e, d in o.cdeps.items():
                    key = "p" + e
                    if w.get(key, 0) < d.count:
                        engobj.wait_ge(psem[e], d.count)
                        w[key] = d.count
                for d in o.ddeps:
                    key = d.sem.name
                    if w.get(key, 0) < d.semval:
                        engobj.wait_ge(d.sem, d.semval)
                        w[key] = d.semval
                ins = o.fn(engobj)
                if o.dma:
                    ins.then_inc(o.sem, 16)
                elif o.marked and ename in psem:
                    ins.then_inc(psem[ename], 1)

        with nc.Block() as block:
            if per["sp"]:
                block.sync(lambda e: emit("sp", e))
            if per["pe"]:
                block.tensor(lambda e: emit("pe", e))
            if per["act"]:
                block.scalar(lambda e: emit("act", e))
            if per["dve"]:
                block.vector(lambda e: emit("dve", e))
            if per["pool"]:
                block.gpsimd(lambda e: emit("pool", e))
        for e, o in lastc.items():
            self.barrier_c[e] = o
        bd = []
        for q in DMA_R:
            bd.extend(o for o in self.last_dma[q] if o is not None)
        self.barrier_d = bd
        self.gen += 1

    def finish(self):
        for e in ("sp", "act", "dve", "pool"):
            self.op(e, lambda g: g.nop())
        self.flush()


class KB:
    def __init__(self, nc, es, dbg):
        self.nc = nc
        self.es = es
        self.S = Sched(nc, es)
        self.dbg = set(dbg)
        self.pes = None
        self.cnt = 0
        self.pb = [Tile(es.enter_context(nc.psum_tensor("pb%d" % i, [128, 512], F32)), "pb%d" % i) for i in range(8)]
        for b in self.pb:
            b.buf.excl = True
        self.rot = list(range(8))
        self.roti = 0

    def name(self, p):
        self.cnt += 1
        return "%s_%d" % (p, self.cnt)

    def phase(self):
        kb = self

        class _P:
            def __enter__(s):
                kb.pes = ExitStack()
                kb.pes.__enter__()
                kb.rot = list(range(8))
                kb.roti = 0
                return kb

            def __exit__(s, *a):
                if a[0] is None:
                    kb.S.flush()
                kb.pes.__exit__(*a)
                kb.pes = None
                return False
        return _P()

    def sb(self, shape, dt, name="t", glob=False):
        st = self.es if glob else self.pes
        n = self.name(name)
        return Tile(st.enter_context(self.nc.sbuf_tensor(n, list(shape), dt)), n)

    def ring(self, n, shape, dt, name="r"):
        return [self.sb(shape, dt, name) for _ in range(n)]

    def dram(self, name, shape, dt, kind=None):
        if kind is None:
            kind = "ExternalOutput" if name in self.dbg else "Internal"
        return self.nc.dram_tensor(name, list(shape), dt, kind=kind).ap()

    def bank(self):
        b = self.pb[self.rot[self.roti % len(self.rot)]]
        self.roti += 1
        return b

    def v(self, eng, meth, reads, writes, **kw):
        self.S.op(eng, lambda e: getattr(e, meth)(**kw), reads, writes)

    def dma(self, q, out, in_, reads=(), writes=()):
        self.S.op(q, lambda e: e.dma_start(out=out, in_=in_), reads, writes, dma=True)

    def mm(self, ot, out, lt, lhsT, rt, rhs, start, stop):
        self.S.op("pe", lambda e: e.matmul(out, lhsT=lhsT, rhs=rhs, start=start, stop=stop), [lt, rt], [ot])

    def tr(self, ot, out, it, in_, idt):
        self.S.op("pe", lambda e: e.transpose(out=out, in_=in_, identity=idt[:]), [it, idt], [ot])


def bcast_rows(ap, n):
    aps = [list(x) for x in ap.ap]
    while len(aps) > 1 and aps[0][1] == 1:
        aps = aps[1:]
    return bass.AP(ap.tensor, ap.offset, [[0, n]] + aps)


def bf(bank):
    return bank[:].bitcast(BF16)


def phase_consts(K, C):
    C.ident = K.sb([128, 128], BF16, "ident", glob=True)
    C.identf = K.sb([128, 128], F32, "identf", glob=True)
    C.ones = K.sb([128, 128], BF16, "ones", glob=True)
    C.onesf = K.sb([128, 128], F32, "onesf", glob=True)
    with K.phase():
        K.dma("sp", C.identf[:], C.cst[:, 0, :], [], [C.identf])
        K.v("dve", "tensor_copy", [C.identf], [C.ident], out=C.ident[:], in_=C.identf[:])
        K.v("pool", "memset", [], [C.onesf], ap=C.onesf[:], constant=1.0)
        K.v("pool", "memset", [], [C.ones], ap=C.ones[:], constant=1.0)


def phase_adaln(K, C, l):
    with K.phase():
        wb = K.ring(2, [128, 16, 512], F32, "wada")
        msb = K.sb([2, 18432], F32, "msb")
        gsb = K.sb([2, 3 * D], F32, "gsb")
        sc = K.sb([128, 16, 2], F32, "sc")
        K.dma("sp", sc[:], C.cT[:, :, :], [], [sc])
        K.v("act", "activation", [sc], [sc], out=sc[:], in_=sc[:], func=AF.Silu)
        K.dma("sp", msb[:], bcast_rows(C.b_ada[l, :], 2), [], [msb])
        K.dma("sp", gsb[:], bcast_rows(C.norm_g[l].rearrange("a d -> (a d)"), 2), [], [gsb])
        wv = C.w_ada[l].rearrange("(kc p) n -> p kc n", p=128)
        for nb in range(36):
            w = wb[nb % 2]
            K.dma("sp", w[:, 0:8, :], wv[:, 0:8, nb * 512:(nb + 1) * 512], [], [w])
            K.dma("act", w[:, 8:16, :], wv[:, 8:16, nb * 512:(nb + 1) * 512], [], [w])
            bk = K.bank()
            for kc in range(16):
                K.mm(bk, bk[0:2, :], sc, sc[:, kc, :], w, w[:, kc, :], kc == 0, kc == 15)
            c0 = nb * 512
            K.v("dve", "tensor_tensor", [bk, msb], [msb], out=msb[:, c0:c0 + 512], in0=bk[0:2, :],
                in1=msb[:, c0:c0 + 512], op=ALU.add)
        for s in range(3):
            a = msb[:, (3 * s + 1) * D:(3 * s + 2) * D]
            K.v("dve", "scalar_tensor_tensor", [msb, gsb], [msb], out=a, in0=a, scalar=1.0,
                in1=gsb[:, s * D:(s + 1) * D], op0=ALU.add, op1=ALU.mult)
        for s in (0, 2):
            g = msb[:, (3 * s + 2) * D:(3 * s + 3) * D]
            K.v("dve", "tensor_scalar", [msb], [msb], out=g, in0=g, scalar1=0.5, scalar2=None, op0=ALU.mult)
        K.dma("sp", C.modr[l], msb[:], [msb], [])


def load_mod(K, C, l, r, j, tile):
    src = C.modr[l][r:r + 1, j * D:(j + 1) * D]
    K.dma("sp", tile[:], bcast_rows(src, 128), [], [tile])


def norm_mod_T(K, C, srcs, Al, Bl, uT, W):
    for i, src in enumerate(srcs):
        A = Al[i]
        Bm = Bl[i]
        hb = W.hb[i % len(W.hb)]
        K.dma("sp", hb[:], src, [], [hb])
        ss = W.st[i % 2]
        K.v("act", "activation", [hb], [W.junk, ss], out=W.junk[:], in_=hb[:], func=AF.Square,
            scale=float(D) ** -0.5, accum_out=ss[:, 0:1])
        K.v("act", "activation", [ss], [ss], out=ss[:, 1:2], in_=ss[:, 0:1], func=AF.Sqrt, bias=EPS, scale=1.0)
        K.v("dve", "reciprocal", [ss], [ss], out=ss[:, 2:3], in_=ss[:, 1:2])
        tmp = W.tmp
        K.v("dve", "scalar_tensor_tensor", [hb, ss, A], [tmp], out=tmp[:], in0=hb[:], scalar=ss[:, 2:3],
            in1=A[:], op0=ALU.mult, op1=ALU.mult)
        ub = W.ub[i % len(W.ub)]
        K.v("dve", "tensor_tensor", [tmp, Bm], [ub], out=ub[:], in0=tmp[:], in1=Bm[:], op=ALU.add)
        for g in range(2):
            bk = K.bank()
            bv = bf(bk)
            for c in range(8):
                cc = g * 8 + c
                K.tr(bk, bv[:, c * 128:(c + 1) * 128], ub, ub[:, cc * 128:(cc + 1) * 128], C.ident)
            K.v("act" if g == 0 else "dve", "copy" if g == 0 else "tensor_copy", [bk], [uT],
                out=uT[:, g * 8:(g + 1) * 8, i * 128:(i + 1) * 128],
                in_=bv.rearrange("p (c t) -> p c t", c=8))


def split_tok(n):
    if n <= 512:
        return [(0, n)]
    h = n // 2
    return [(0, h), (h, n - h)]


class NS:
    pass


def phase_ffn(K, C, l, s, wg, wu, wd, blocks, src, dst):
    with K.phase():
        W = NS()
        W.hb = K.ring(2, [128, D], F32, "hb")
        W.st = K.ring(2, [128, 4], F32, "st")
        W.ub = K.ring(2, [128, D], BF16, "ub")
        Ar = [K.sb([128, D], F32, "A") for _ in range(2)]
        Br = [K.sb([128, D], F32, "B") for _ in range(2)]
        uT = K.sb([128, 16, 768], BF16, "uT")
        aT = K.sb([128, 44, 768], BF16, "aT")
        wgt = K.ring(2, [128, 16, 128], BF16, "wg")
        wut = K.ring(2, [128, 16, 128], BF16, "wu")
        wdt = K.ring(2, [128, 11, 512], BF16, "wd")
        sg = K.ring(2, [128, 512], F32, "sg")
        hs = K.ring(2, [128, 512], F32, "hs")
        ho = K.ring(2, [128, 512], F32, "ho")
        wgv = wg.rearrange("(kc p) n -> p kc n", p=128)
        wuv = wu.rearrange("(kc p) n -> p kc n", p=128)
        wdv = wd.rearrange("(j p) n -> p j n", p=128)
        W.tmp = K.sb([128, D], F32, "tmp")
        W.junk = W.tmp
        nd = 0
        for tiles in blocks:
            n = len(tiles)
            ntok = n * 128
            rs = sorted(set(1 if t < 2 else 0 for t in tiles))
            for r in rs:
                load_mod(K, C, l, r, 3 * s + 1, Ar[r])
                load_mod(K, C, l, r, 3 * s + 0, Br[r])
            K.rot = list(range(8))
            norm_mod_T(K, C, [src(t) for t in tiles], [Ar[1 if t < 2 else 0] for t in tiles],
                       [Br[1 if t < 2 else 0] for t in tiles], uT, W)
            for r in rs:
                load_mod(K, C, l, r, 3 * s + 2, Ar[r])
            halves = split_tok(ntok)
            for j in range(44):
                a = wgt[j % 2]
                b = wut[j % 2]
                K.dma("pool", a[:], wgv[:, :, j * 128:(j + 1) * 128], [], [a])
                K.dma("pool", b[:], wuv[:, :, j * 128:(j + 1) * 128], [], [b])
                for (t0, nn) in halves:
                    pg = K.bank()
                    pu = K.bank()
                    for kc in range(16):
                        K.mm(pg, pg[:, 0:nn], a, a[:, kc, :], uT, uT[:, kc, t0:t0 + nn], kc == 0, kc == 15)
                    for kc in range(16):
                        K.mm(pu, pu[:, 0:nn], b, b[:, kc, :], uT, uT[:, kc, t0:t0 + nn], kc == 0, kc == 15)
                    sgt = sg[nd % 2]
                    nd += 1
                    K.v("act", "activation", [pg], [sgt], out=sgt[:, 0:nn], in_=pg[:, 0:nn], func=AF.Silu)
                    K.v("dve", "tensor_tensor", [sgt, pu], [aT], out=aT[:, j, t0:t0 + nn], in0=sgt[:, 0:nn],
                        in1=pu[:, 0:nn], op=ALU.mult)
            qi = 0
            for nb in range(4):
                for q in range(4):
                    wt = wdt[qi % 2]
                    qi += 1
                    K.dma("pool", wt[:], wdv[:, q * 11:(q + 1) * 11, nb * 512:(nb + 1) * 512], [], [wt])
                    for i in range(n):
                        bk = K.pb[i]
                        for jj in range(11):
                            j = q * 11 + jj
                            K.mm(bk, bk[:, :], aT, aT[:, j, i * 128:(i + 1) * 128], wt, wt[:, jj, :], j == 0, j == 43)
                for i, t in enumerate(tiles):
                    bk = K.pb[i]
                    h1 = hs[(nb * n + i) % 2]
                    o1 = ho[(nb * n + i) % 2]
                    G = Ar[1 if t < 2 else 0]
                    K.dma("sp", h1[:], src(t)[:, nb * 512:(nb + 1) * 512], [], [h1])
                    K.v("dve", "tensor_tensor", [bk, G], [o1], out=o1[:], in0=bk[:, :], in1=G[:, nb * 512:(nb + 1) * 512],
                        op=ALU.mult)
                    K.v("dve", "tensor_tensor", [o1, h1], [o1], out=o1[:], in0=o1[:], in1=h1[:], op=ALU.add)
                    K.dma("act", dst(t)[:, nb * 512:(nb + 1) * 512], o1[:], [o1], [])


def rms_groups(K, X, xv, G, Dg, gain_bc, Y, yv, W):
    if isinstance(W.sq, list):
        W.rmsi = getattr(W, "rmsi", 0) + 1
        sq = W.sq[W.rmsi % len(W.sq)]
        st = W.gst[W.rmsi % len(W.gst)]
    else:
        sq = W.sq
        st = W.gst
    sqv = sq[:, 0:G * Dg].rearrange("p (g d) -> p g d", g=G)
    K.v("dve", "tensor_tensor", [X], [sq], out=sqv, in0=xv, in1=xv, op=ALU.mult)
    K.v("dve", "tensor_reduce", [sq], [st], out=st[:, 0:G], in_=sqv, axis=AX.X, op=ALU.add)
    K.v("act", "activation", [st], [st], out=st[:, 8:8 + G], in_=st[:, 0:G], func=AF.Sqrt, bias=EPS, scale=1.0 / Dg)
    K.v("dve", "reciprocal", [st], [st], out=st[:, 16:16 + G], in_=st[:, 8:8 + G])
    K.v("dve", "tensor_tensor", [X, st], [sq], out=sqv, in0=xv,
        in1=st[:, 16:16 + G].unsqueeze(2).to_broadcast([128, G, Dg]), op=ALU.mult)
    K.v("dve", "tensor_tensor", [sq, W.gains], [Y], out=yv, in0=sqv, in1=gain_bc, op=ALU.mult)


def rope_cast(K, Y, yv, H, Ob, ov, cs, W, latent):
    if not latent:
        K.v("act", "copy", [Y], [Ob], out=ov, in_=yv)
        return
    K.v("act", "copy", [Y], [Ob], out=ov[:, :, 0:128], in_=yv[:, :, 0:128])
    yr = yv[:, :, 128:192].rearrange("p h (a t f) -> p h a t f", a=2, t=2)
    orr = ov[:, :, 128:192].rearrange("p h (a t f) -> p h a t f", a=2, t=2)
    x1 = yr[:, :, :, 0, :]
    x2 = yr[:, :, :, 1, :]
    cb = cs[:, 0:32].rearrange("p (a f) -> p a f", a=2).unsqueeze(1).to_broadcast([128, H, 2, 16])
    sb_ = cs[:, 32:64].rearrange("p (a f) -> p a f", a=2).unsqueeze(1).to_broadcast([128, H, 2, 16])
    W.rti = getattr(W, "rti", 0) + 1
    rt = W.rt[W.rti % len(W.rt)] if isinstance(W.rt, list) else W.rt
    tv = [rt[:, k * 256:k * 256 + H * 32].rearrange("p (h a f) -> p h a f", h=H, a=2) for k in range(4)]
    K.v("dve", "tensor_tensor", [Y, cs], [rt], out=tv[0], in0=x1, in1=cb, op=ALU.mult)
    K.v("dve", "tensor_tensor", [Y, cs], [rt], out=tv[1], in0=x2, in1=sb_, op=ALU.mult)
    K.v("dve", "tensor_tensor", [Y, cs], [rt], out=tv[2], in0=x2, in1=cb, op=ALU.mult)
    K.v("dve", "tensor_tensor", [Y, cs], [rt], out=tv[3], in0=x1, in1=sb_, op=ALU.mult)
    K.v("dve", "tensor_tensor", [rt], [Ob], out=orr[:, :, :, 0, :], in0=tv[0], in1=tv[1], op=ALU.subtract)
    K.v("dve", "tensor_tensor", [rt], [Ob], out=orr[:, :, :, 1, :], in0=tv[2], in1=tv[3], op=ALU.add)


def phase_win(K, C, l, blocks, src):
    with K.phase():
        W = NS()
        scr = K.sb([128, 13312], F32, "scr")

        def sub(a, b, dt=F32, name="s"):
            ap = scr.t[:, a:b]
            if dt == BF16:
                ap = ap.bitcast(BF16)
            return Tile(ap, name)
        W.hb = [sub(0, 2048), K.sb([128, D], F32, "hb2")]
        W.tmp = sub(2048, 4096)
        W.junk = W.tmp
        W.ub = [sub(4096, 5120, BF16), K.sb([128, D], BF16, "ub2")]
        Ar = [sub(5120, 7168), sub(9216, 11264)]
        Br = [sub(7168, 9216), sub(11264, 13312)]
        Xs = [sub(i * 1536, (i + 1) * 1536) for i in range(6)]
        W.sq = [sub(9216, 10752), sub(10752, 12288)]
        W.rt = [sub(12288, 13312), K.sb([128, 1024], F32, "rt1")]
        W.st = K.ring(2, [128, 4], F32, "st")
        W.gst = K.ring(2, [128, 24], F32, "gst")
        uT = K.sb([128, 16, 768], BF16, "uT")
        wr = K.ring(2, [128, 16, 512], BF16, "win")
        wuk = K.sb([128, 4, 1024], BF16, "wuk")
        wuv = K.sb([128, 4, 1024], BF16, "wuv")
        gains = K.sb([128, 1184], F32, "gains")
        W.gains = gains
        Yr = K.ring(2, [128, 1536], F32, "Y")
        Obs = [K.sb([128, 1536], BF16, "Ob") for _ in range(6)]
        ckTs = [K.sb([128, 4, 128], BF16, "ckT") for _ in range(6)]
        stg = K.ring(2, [128, 16, 128], BF16, "stg")
        kr = K.ring(6, [128, 64], F32, "kr")
        o32 = K.ring(2, [128, 768], F32, "o32")
        o16 = K.ring(2, [128, 1024], BF16, "o16")
        csr = K.ring(6, [128, 64], F32, "cs")
        sp = K.sb([128, 5, 32], F32, "sp")
        K.dma("sp", gains[:, 0:192], bcast_rows(C.mla_q_norm_g[l, :], 128), [], [gains])
        K.dma("sp", gains[:, 192:384], bcast_rows(C.mla_k_norm_g[l, :], 128), [], [gains])
        K.dma("sp", gains[:, 384:896], bcast_rows(C.mla_kv_norm_g[l, :], 128), [], [gains])
        K.dma("sp", gains[:, 896:1024], bcast_rows(C.na_q_norm_g[l, :], 128), [], [gains])
        K.dma("sp", gains[:, 1024:1152], bcast_rows(C.na_k_norm_g[l, :], 128), [], [gains])
        K.dma("sp", gains[:, 1152:1184], bcast_rows(C.ssd_dt_bias[l].rearrange("a h -> (a h)"), 128), [], [gains])
        K.dma("pool", wuk[:], C.mla_w_uk[l].rearrange("(c p) n -> p c n", p=128), [], [wuk])
        K.dma("pool", wuv[:], C.mla_w_uv[l].rearrange("(c p) n -> p c n", p=128), [], [wuv])
        wv = C.w_in[l].rearrange("(kc p) n -> p kc n", p=128)
        cnt = NS()
        cnt.wi = 0
        cnt.oi = 0

        for tiles in blocks:
            n = len(tiles)
            ntok = n * 128
            rs = sorted(set(1 if t < 2 else 0 for t in tiles))
            for r in rs:
                load_mod(K, C, l, r, 4, Ar[r])
                load_mod(K, C, l, r, 3, Br[r])
            K.rot = list(range(8))
            norm_mod_T(K, C, [src(t) for t in tiles], [Ar[1 if t < 2 else 0] for t in tiles],
                       [Br[1 if t < 2 else 0] for t in tiles], uT, W)
            K.S.flush()
            cst = {}
            for i, t in enumerate(tiles):
                if t >= 2:
                    cst[i] = csr[i % 6]
                    K.dma("sp", cst[i][:], C.rope[(t - 2) * 128:(t - 1) * 128, :], [], [cst[i]])
            pairs = [list(range(a, min(a + 2, n))) for a in range(0, n, 2)]

            def load_w(c0, nc_):
                w = wr[cnt.wi % 2]
                cnt.wi += 1
                K.dma("pool", w[:, :, 0:nc_], wv[:, :, c0:c0 + nc_], [], [w])
                return w

            def proj(w, nc_, i):
                bk = K.bank()
                for kc in range(16):
                    K.mm(bk, bk[:, 0:nc_], uT, uT[:, kc, i * 128:(i + 1) * 128], w, w[:, kc, 0:nc_], kc == 0, kc == 15)
                return bk

            def seg_to_x(c0, nc_, x0):
                w = load_w(c0, nc_)
                for i in range(n):
                    bk = proj(w, nc_, i)
                    K.v("act", "copy", [bk], [Xs[i]], out=Xs[i][:, x0:x0 + nc_], in_=bk[:, 0:nc_])

            def easy(kind, c0, nc_):
                w = load_w(c0, nc_)
                if kind == "xbc":
                    for ci in range(4):
                        o = o32[cnt.oi % 2]
                        cnt.oi += 1
                        for (t0, nn) in split_tok(ntok):
                            bk = K.bank()
                            for kc in range(16):
                                K.mm(bk, bk[:, 0:nn], w, w[:, kc, ci * 128:(ci + 1) * 128], uT, uT[:, kc, t0:t0 + nn], kc == 0, kc == 15)
                            K.v("act", "copy", [bk], [o], out=o[:, t0:t0 + nn], in_=bk[:, 0:nn])
                        ch0 = c0 - 3136 + ci * 128
                        K.dma("sp", C.xbcT[ch0:ch0 + 128, tiles[0] * 128:tiles[0] * 128 + ntok], o[:, 0:ntok], [o], [])
                    return
                for i, t in enumerate(tiles):
                    tok = t * 128
                    bk = proj(w, nc_, i)
                    if kind == "kr":
                        K.v("act", "copy", [bk], [kr[i]], out=kr[i][:], in_=bk[:, 0:64])
                    elif kind in ("z", "g"):
                        o = o32[cnt.oi % 2]
                        cnt.oi += 1
                        K.v("act", "activation", [bk], [o], out=o[:, 0:512], in_=bk[:, :], func=(AF.Silu if kind == "z" else AF.Sigmoid))
                        if kind == "z":
                            K.dma("act", C.zs[tok:tok + 128, c0 - 2112:c0 - 2112 + 512], o[:, 0:512], [o], [])
                        else:
                            K.dma("act", C.gts[tok:tok + 128, c0 - 8288:c0 - 8288 + 512], o[:, 0:512], [o], [])
                    elif kind == "nv":
                        o = o16[cnt.oi % 2]
                        cnt.oi += 1
                        K.v("act", "copy", [bk], [o], out=o[:, 0:512], in_=bk[:, :])
                        K.dma("act", C.nv[tok:tok + 128, c0 - 7264:c0 - 7264 + 512], o[:, 0:512], [o], [])
                    elif kind == "dt":
                        K.v("dve", "tensor_tensor", [bk, gains], [sp], out=sp[:, 0, :], in0=bk[:, 0:32], in1=gains[:, 1152:1184], op=ALU.add)
                        K.v("act", "activation", [sp], [sp], out=sp[:, 1, :], in_=sp[:, 0, :], func=AF.Abs)
                        K.v("act", "activation", [sp], [sp], out=sp[:, 2, :], in_=sp[:, 1, :], func=AF.Exp, scale=-1.0)
                        K.v("act", "activation", [sp], [sp], out=sp[:, 3, :], in_=sp[:, 2, :], func=AF.Ln, bias=1.0, scale=1.0)
                        K.v("dve", "tensor_scalar_max", [sp], [sp], out=sp[:, 1, :], in0=sp[:, 0, :], scalar1=0.0)
                        K.v("dve", "tensor_tensor", [sp], [sp], out=sp[:, 4, :], in0=sp[:, 1, :], in1=sp[:, 3, :], op=ALU.add)
                        K.dma("sp", C.dts[tok:tok + 128, :], sp[:, 4, :], [sp], [])

            def rms_multi(grp, G, Dg, gain_bc, outs):
                xv = [Xs[i][:, 0:G * Dg].rearrange("p (g d) -> p g d", g=G) for i in grp]
                sq = [W.sq[k % 2] for k in range(len(grp))]
                sqv = [q_[:, 0:G * Dg].rearrange("p (g d) -> p g d", g=G) for q_ in sq]
                st = [W.gst[k % 2] for k in range(len(grp))]
                for k, i in enumerate(grp):
                    K.v("dve", "tensor_tensor", [Xs[i]], [sq[k]], out=sqv[k], in0=xv[k], in1=xv[k], op=ALU.mult)
                for k, i in enumerate(grp):
                    K.v("dve", "tensor_reduce", [sq[k]], [st[k]], out=st[k][:, 0:G], in_=sqv[k], axis=AX.X, op=ALU.add)
                for k, i in enumerate(grp):
                    K.v("act", "activation", [st[k]], [st[k]], out=st[k][:, 8:8 + G], in_=st[k][:, 0:G], func=AF.Sqrt, bias=EPS, scale=1.0 / Dg)
                for k, i in enumerate(grp):
                    K.v("dve", "reciprocal", [st[k]], [st[k]], out=st[k][:, 16:16 + G], in_=st[k][:, 8:8 + G])
                for k, i in enumerate(grp):
                    K.v("dve", "tensor_tensor", [Xs[i], st[k]], [sq[k]], out=sqv[k], in0=xv[k],
                        in1=st[k][:, 16:16 + G].unsqueeze(2).to_broadcast([128, G, Dg]), op=ALU.mult)
                for k, i in enumerate(grp):
                    K.v("dve", "tensor_tensor", [sq[k], gains], [outs[k][0]], out=outs[k][1], in0=sqv[k], in1=gain_bc, op=ALU.mult)

            def rope_multi(grp, Ys):
                H = 8
                yv = [y[:, :].rearrange("p (h d) -> p h d", h=H) for y in Ys]
                ov = [Obs[i][:, :].rearrange("p (h d) -> p h d", h=H) for i in grp]
                lat = [tiles[i] >= 2 for i in grp]
                for k, i in enumerate(grp):
                    if lat[k]:
                        K.v("act", "copy", [Ys[k]], [Obs[i]], out=ov[k][:, :, 0:128], in_=yv[k][:, :, 0:128])
                    else:
                        K.v("act", "copy", [Ys[k]], [Obs[i]], out=ov[k], in_=yv[k])
                ks = [k for k in range(len(grp)) if lat[k]]
                if not ks:
                    return
                yr = {k: yv[k][:, :, 128:192].rearrange("p h (a t f) -> p h a t f", a=2, t=2) for k in ks}
                orr = {k: ov[k][:, :, 128:192].rearrange("p h (a t f) -> p h a t f", a=2, t=2) for k in ks}
                cb = {k: cst[grp[k]][:, 0:32].rearrange("p (a f) -> p a f", a=2).unsqueeze(1).to_broadcast([128, H, 2, 16]) for k in ks}
                sb_ = {k: cst[grp[k]][:, 32:64].rearrange("p (a f) -> p a f", a=2).unsqueeze(1).to_broadcast([128, H, 2, 16]) for k in ks}
                rt = {k: W.rt[k % 2] for k in ks}
                tv = {k: [rt[k][:, j * 256:j * 256 + H * 32].rearrange("p (h a f) -> p h a f", h=H, a=2) for j in range(4)] for k in ks}
                for (j, xi, tb) in ((0, 0, cb), (1, 1, sb_), (2, 1, cb), (3, 0, sb_)):
                    for k in ks:
                        K.v("dve", "tensor_tensor", [Ys[k], cst[grp[k]]], [rt[k]], out=tv[k][j], in0=yr[k][:, :, :, xi, :], in1=tb[k], op=ALU.mult)
                for k in ks:
                    K.v("dve", "tensor_tensor", [rt[k]], [Obs[grp[k]]], out=orr[k][:, :, :, 0, :], in0=tv[k][0], in1=tv[k][1], op=ALU.subtract)
                for k in ks:
                    K.v("dve", "tensor_tensor", [rt[k]], [Obs[grp[k]]], out=orr[k][:, :, :, 1, :], in0=tv[k][2], in1=tv[k][3], op=ALU.add)

            def heads_A(g0):
                for grp in pairs:
                    Ys = [Yr[k % 2] for k in range(len(grp))]
                    outs = [(Ys[k], Ys[k][:, :].rearrange("p (h d) -> p h d", h=8)) for k in range(len(grp))]
                    rms_multi(grp, 8, 192, gains[:, g0:g0 + 192].unsqueeze(1).to_broadcast([128, 8, 192]), outs)
                    rope_multi(grp, Ys)

            def heads_B(dstT):
                for i, t in enumerate(tiles):
                    tok = t * 128
                    ov8 = Obs[i][:, :].rearrange("p (h d) -> p h d", h=8)
                    ba = K.bank()
                    bb = K.bank()
                    bav = bf(ba)
                    bbv = bf(bb)
                    for hh in range(8):
                        K.tr(ba, bav[:, hh * 128:(hh + 1) * 128], Obs[i], ov8[:, hh, 0:128], C.ident)
                    for hh in range(8):
                        K.tr(bb, bbv[0:64, hh * 128:(hh + 1) * 128], Obs[i], ov8[:, hh, 128:192], C.ident)
                    sg_ = stg[cnt.oi % 2]
                    cnt.oi += 1
                    K.v("act", "copy", [ba], [sg_], out=sg_[:, 0:8, :], in_=bav.rearrange("p (h t) -> p h t", h=8))
                    K.v("dve", "tensor_copy", [bb], [sg_], out=sg_[0:64, 8:16, :], in_=bbv[0:64, :].rearrange("p (h t) -> p h t", h=8))
                    K.dma("sp", dstT[:, 0:128, tok:tok + 128].rearrange("h d t -> d h t"), sg_[:, 0:8, :], [sg_], [])
                    K.dma("sp", dstT[:, 128:192, tok:tok + 128].rearrange("h d t -> d h t"), sg_[0:64, 8:16, :], [sg_], [])

            def na_A(g0):
                for grp in pairs:
                    outs = [(Obs[i], Obs[i][:, 0:1024].rearrange("p (h d) -> p h d", h=8)) for i in grp]
                    rms_multi(grp, 8, 128, gains[:, g0:g0 + 128].unsqueeze(1).to_broadcast([128, 8, 128]), outs)

            def na_B(dstT):
                for i, t in enumerate(tiles):
                    tok = t * 128
                    b2 = K.bank()
                    bv = bf(b2)
                    for hh in range(8):
                        K.tr(b2, bv[:, hh * 128:(hh + 1) * 128], Obs[i], Obs[i][:, hh * 128:(hh + 1) * 128], C.ident)
                    sg_ = stg[cnt.oi % 2]
                    cnt.oi += 1
                    K.v("act", "copy", [b2], [sg_], out=sg_[:, 0:8, :], in_=bv.rearrange("p (h t) -> p h t", h=8))
                    K.dma("sp", dstT[:, :, tok:tok + 128].rearrange("h d t -> d h t"), sg_[:, 0:8, :], [sg_], [])

            gsegs = [("g", 8288 + 512 * i, 512) for i in range(12)]
            easy("kr", 2048, 64)
            seg_to_x(1536, 512, 0)
            for grp in pairs:
                outs = [(Obs[i], Obs[i][:, 0:512].rearrange("p (g d) -> p g d", g=1)) for i in grp]
                rms_multi(grp, 1, 512, gains[:, 384:896].unsqueeze(1), outs)
            easy("z", 2112, 512)
            easy("z", 2624, 512)
            easy(*gsegs[0])
            for i in range(n):
                b2 = K.bank()
                bv = bf(b2)
                for c in range(4):
                    K.tr(b2, bv[:, c * 128:(c + 1) * 128], Obs[i], Obs[i][:, c * 128:(c + 1) * 128], C.ident)
                K.v("act", "copy", [b2], [ckTs[i]], out=ckTs[i][:], in_=bv[:, 0:512].rearrange("p (h t) -> p h t", h=4))
            for i, t in enumerate(tiles):
                ckT = ckTs[i]
                KF = Xs[i][:, :].rearrange("p (h d) -> p h d", h=8)
                for nb2 in range(2):
                    b3 = K.bank()
                    for c in range(4):
                        K.mm(b3, b3[:, :], ckT, ckT[:, c, :], wuk, wuk[:, c, nb2 * 512:(nb2 + 1) * 512], c == 0, c == 3)
                    K.v("act", "copy", [b3], [Xs[i]], out=KF[:, 4 * nb2:4 * nb2 + 4, 0:128],
                        in_=b3[:, :].rearrange("p (h d) -> p h d", h=4))
                K.v("dve", "tensor_copy", [kr[i]], [Xs[i]], out=KF[:, :, 128:192],
                    in_=kr[i][:].unsqueeze(1).to_broadcast([128, 8, 64]))
                o = o16[cnt.oi % 2]
                cnt.oi += 1
                for nb2 in range(2):
                    b3 = K.bank()
                    for c in range(4):
                        K.mm(b3, b3[:, :], ckT, ckT[:, c, :], wuv, wuv[:, c, nb2 * 512:(nb2 + 1) * 512], c == 0, c == 3)
                    K.v("act", "copy", [b3], [o], out=o[:, nb2 * 512:(nb2 + 1) * 512], in_=b3[:, :])
                K.dma("sp", C.vm[t * 128:(t + 1) * 128, :], o[:], [o], [])
            heads_A(192)
            easy("dt", 5184, 32)
            easy("nv", 7264, 512)
            easy("nv", 7776, 512)
            easy(*gsegs[1])
            easy(*gsegs[2])
            heads_B(C.kT)
            for qi in range(4):
                seg_to_x(384 * qi, 384, 384 * qi)
            heads_A(0)
            for gsg in gsegs[3:7]:
                easy(*gsg)
            heads_B(C.qT)
            seg_to_x(5216, 512, 0)
            seg_to_x(5728, 512, 512)
            na_A(896)
            for gsg in gsegs[7:10]:
                easy(*gsg)
            na_B(C.nqT)
            seg_to_x(6240, 512, 0)
            seg_to_x(6752, 512, 512)
            na_A(1024)
            for gsg in gsegs[10:12]:
                easy(*gsg)
            easy("xbc", 3136, 512)
            easy("xbc", 3648, 512)
            na_B(C.nkT)
            easy("xbc", 4160, 512)
            easy("xbc", 4672, 512)
            K.S.flush()


def attn_full(K, C, qn, qr, kn, kr_, V, q0, nq, kbs, scale, dstT, W):
    pO = K.pb[W.acc]
    pD = K.pb[W.acc + 1]
    W.acc = 4 + (W.acc - 4 + 2) % 4
    n = len(kbs)

    def qk(ki):
        kb = kbs[ki]
        pS = K.bank()
        K.mm(pS, pS[:, 0:nq], kn, kn[:, kb * 128:(kb + 1) * 128], qn, qn[:, q0:q0 + nq], True, qr is None)
        if qr is not None:
            K.mm(pS, pS[:, 0:nq], kr_, kr_[0:64, kb * 128:(kb + 1) * 128], qr, qr[0:64, q0:q0 + nq], False, True)
        P = W.P[W.pi % len(W.P)]
        W.pi += 1
        K.v("act", "activation", [pS], [P], out=P[:, 0:nq], in_=pS[:, 0:nq], func=AF.Exp, scale=scale)
        return P
    Pn = qk(0)
    for ki, kb in enumerate(kbs):
        P = Pn
        if ki + 1 < n:
            Pn = qk(ki + 1)
        K.mm(pO, pO[:, 0:nq], V, V[:, kb, :], P, P[:, 0:nq], ki == 0, ki == n - 1)
        K.mm(pD, pD[:, 0:nq], C.ones, C.ones[:], P, P[:, 0:nq], ki == 0, ki == n - 1)
    rd = W.rd[W.pi % 2]
    ob = W.ob[W.pi % 2]
    K.v("dve", "reciprocal", [pD], [rd], out=rd[:, 0:nq], in_=pD[:, 0:nq])
    K.v("dve", "tensor_tensor", [pO, rd], [ob], out=ob[:, 0:nq], in0=pO[:, 0:nq], in1=rd[:, 0:nq], op=ALU.mult)
    K.dma("sp", dstT[:, q0:q0 + nq], ob[:, 0:nq], [ob], [])


def phase_mla(K, C, l, need_ctx):
    with K.phase():
        W = NS()
        W.P = K.ring(4, [128, 512], BF16, "P")
        W.rd = K.ring(2, [128, 512], F32, "rd")
        W.ob = K.ring(2, [128, 512], BF16, "ob")
        W.pi = 0
        W.acc = 4
        K.rot = [0, 1, 2, 3]
        qn = K.ring(2, [128, NTOK], BF16, "qn")
        qr = K.ring(2, [64, NTOK], BF16, "qr")
        kn = K.ring(2, [128, NTOK], BF16, "kn")
        kr_ = K.ring(2, [64, NTOK], BF16, "kr")
        Vt = K.ring(2, [128, 18, 128], BF16, "V")
        scale = 192.0 ** -0.5
        for h in range(8):
            a, b, c, d, V = qn[h % 2], qr[h % 2], kn[h % 2], kr_[h % 2], Vt[h % 2]
            K.dma("sp", a[:], C.qT[h, 0:128, :], [], [a])
            K.dma("sp", b[:], C.qT[h, 128:192, :], [], [b])
            K.dma("sp", c[:], C.kT[h, 0:128, :], [], [c])
            K.dma("sp", d[:], C.kT[h, 128:192, :], [], [d])
            for vv in range(3):
                K.dma("sp", V[:, vv * 6:(vv + 1) * 6, :],
                      C.vm[vv * 768:(vv + 1) * 768, h * 128:(h + 1) * 128].rearrange("(kb p) d -> p kb d", p=128), [], [V])
            dstT = C.brT[0, h * 128:(h + 1) * 128, :]
            if need_ctx:
                attn_full(K, C, a, b, c, d, V, 0, 256, [0, 1], scale, dstT, W)
            for qb in range(4):
                attn_full(K, C, a, b, c, d, V, 256 + qb * 512, 512, list(range(18)), scale, dstT, W)


def na_band(i):
    kb10 = min(max(2 * i - 4, 0), 22)
    var = {0: 0, 1: 1, 14: 3, 15: 4}.get(i, 2)
    return kb10, var


def phase_na(K, C, l, need_ctx):
    with K.phase():
        W = NS()
        W.P = K.ring(3, [128, 512], BF16, "P")
        W.rd = K.ring(2, [128, 512], F32, "rd")
        W.ob = K.ring(2, [128, 512], BF16, "ob")
        W.pi = 0
        W.acc = 4
        K.rot = [0, 1, 2, 3]
        qn = K.ring(2, [128, NTOK], BF16, "nq")
        kn = K.ring(2, [128, NTOK], BF16, "nk")
        Vt = K.ring(2, [128, 18, 128], BF16, "nV")
        bt = K.ring(2, [128, 25, 128], F32, "nab")
        La = K.ring(2, [128, 640], F32, "La")
        Pn = K.ring(3, [128, 7, 128], BF16, "Pn")
        o4 = K.ring(2, [128, 512], BF16, "o4")
        rdn = K.ring(2, [128, 128], F32, "rdn")
        scale = 128.0 ** -0.5
        li = 0
        for h in range(8):
            a, c, V, bia = qn[h % 2], kn[h % 2], Vt[h % 2], bt[h % 2]
            K.dma("sp", a[:], C.nqT[h, :, :], [], [a])
            K.dma("sp", c[:], C.nkT[h, :, :], [], [c])
            for vv in range(3):
                K.dma("sp", V[:, vv * 6:(vv + 1) * 6, :],
                      C.nv[vv * 768:(vv + 1) * 768, h * 128:(h + 1) * 128].rearrange("(kb p) d -> p kb d", p=128), [], [V])
            for vv in range(5):
                K.dma("sp", bia[:, vv * 5:(vv + 1) * 5, :], C.nab[l, h, vv * 5:(vv + 1) * 5].rearrange("v k q -> k v q"), [], [bia])
            dstT = C.brT[2, h * 128:(h + 1) * 128, :]
            if need_ctx:
                attn_full(K, C, a, None, c, None, V, 0, 256, [0, 1], scale, dstT, W)
            def st1(i):
                kb10, var = na_band(i)
                qtok = 256 + i * 128
                kblk0 = 2 + kb10 // 2
                pA = K.bank()
                pB = K.bank()
                for j in range(4):
                    K.mm(pA, pA[:, j * 128:(j + 1) * 128], c, c[:, (kblk0 + j) * 128:(kblk0 + j + 1) * 128], a, a[:, qtok:qtok + 128], True, True)
                K.mm(pB, pB[:, 0:128], c, c[:, (kblk0 + 4) * 128:(kblk0 + 5) * 128], a, a[:, qtok:qtok + 128], True, True)
                for cc in range(2):
                    K.mm(pB, pB[:, 128 + cc * 128:256 + cc * 128], c, c[:, cc * 128:(cc + 1) * 128], a, a[:, qtok:qtok + 128], True, True)
                la = La[st1.n % 2]
                P = Pn[st1.n % 3]
                st1.n += 1
                bv = bia[:, var * 5:var * 5 + 5, :]
                K.v("dve", "scalar_tensor_tensor", [pA, bia], [la], out=la[:, 0:512], in0=pA[:, :], scalar=scale,
                    in1=bv[:, 0:4, :].rearrange("p j q -> p (j q)"), op0=ALU.mult, op1=ALU.add)
                K.v("dve", "scalar_tensor_tensor", [pB, bia], [la], out=la[:, 512:640], in0=pB[:, 0:128], scalar=scale,
                    in1=bv[:, 4, :], op0=ALU.mult, op1=ALU.add)
                Pf = P[:].rearrange("p j q -> p (j q)")
                K.v("act", "activation", [la], [P], out=Pf[:, 0:640], in_=la[:, :], func=AF.Exp)
                K.v("act", "activation", [pB], [P], out=Pf[:, 640:896], in_=pB[:, 128:384], func=AF.Exp, scale=scale)
                return P
            st1.n = li

            def st2(i, P):
                kb10, var = na_band(i)
                qtok = 256 + i * 128
                kblk0 = 2 + kb10 // 2
                pO = K.pb[W.acc]
                pD = K.pb[W.acc + 1]
                W.acc = 4 + (W.acc - 4 + 2) % 4
                for j in range(7):
                    vb = kblk0 + j if j < 5 else j - 5
                    K.mm(pO, pO[:, 0:128], V, V[:, vb, :], P, P[:, j, :], j == 0, j == 6)
                    K.mm(pD, pD[:, 0:128], C.ones, C.ones[:], P, P[:, j, :], j == 0, j == 6)
                rd = rdn[i % 2]
                ob = o4[(i // 4) % 2]
                K.v("dve", "reciprocal", [pD], [rd], out=rd[:], in_=pD[:, 0:128])
                K.v("dve", "tensor_tensor", [pO, rd], [ob], out=ob[:, (i % 4) * 128:(i % 4 + 1) * 128], in0=pO[:, 0:128], in1=rd[:], op=ALU.mult)
                if i % 4 == 3:
                    K.dma("sp", dstT[:, qtok - 384:qtok + 128], ob[:], [ob], [])
            Pnext = st1(0)
            for i in range(16):
                Pc = Pnext
                if i + 1 < 16:
                    Pnext = st1(i + 1)
                st2(i, Pc)
            li = st1.n


def phase_merge(K, C, l, blocks, src, dst):
    with K.phase():
        K.rot = list(range(8))
        Gr = [K.sb([128, D], F32, "G") for _ in range(2)]
        bT = [K.sb([128, 8, 768], BF16, "bT") for _ in range(3)]
        wbr = [K.ring(2, [128, 8, 512], BF16, "wbr") for _ in range(3)]
        gt = K.ring(3, [128, 512], F32, "gt")
        mb = [K.sb([128, D], BF16, "mb") for _ in range(6)]
        mT = K.sb([128, 16, 768], BF16, "mT")
        wo = K.ring(2, [128, 16, 512], BF16, "wo")
        hs = K.ring(2, [128, 512], F32, "hs")
        ho = K.ring(2, [128, 512], F32, "ho")
        acc = K.ring(3, [128, 512], F32, "acc")
        tmpg = K.ring(2, [128, 512], F32, "tg")
        wi = 0
        gi = 0
        ai = 0
        for tiles in blocks:
            n = len(tiles)
            ntok = n * 128
            tok0 = tiles[0] * 128
            for r in sorted(set(1 if t < 2 else 0 for t in tiles)):
                load_mod(K, C, l, r, 5, Gr[r])
            for bi in range(3):
                K.dma("sp", bT[bi][:, :, 0:ntok], C.brT[bi].rearrange("(kc p) t -> p kc t", p=128)[:, :, tok0:tok0 + ntok], [], [bT[bi]])
            for nb in range(4):
                ws = []
                for bi in range(3):
                    w = wbr[bi][wi % 2]
                    K.dma("pool", w[:], C.w_branch[l, bi].rearrange("(kc p) n -> p kc n", p=128)[:, :, nb * 512:(nb + 1) * 512], [], [w])
                    ws.append(w)
                wi += 1
                for i, t in enumerate(tiles):
                    ac = acc[ai % 3]
                    ai += 1
                    for bi in range(3):
                        g = gt[gi % 3]
                        K.dma("act", g[:], C.gts[t * 128:(t + 1) * 128, bi * D + nb * 512:bi * D + (nb + 1) * 512], [], [g])
                        bk = K.bank()
                        for kc in range(8):
                            K.mm(bk, bk[:, :], bT[bi], bT[bi][:, kc, i * 128:(i + 1) * 128], ws[bi], ws[bi][:, kc, :], kc == 0, kc == 7)
                        mslice = mb[i][:, nb * 512:(nb + 1) * 512]
                        if bi == 0:
                            K.v("dve", "tensor_tensor", [bk, g], [ac], out=ac[:], in0=bk[:, :], in1=g[:], op=ALU.mult)
                        else:
                            tg = tmpg[gi % 2]
                            K.v("dve", "tensor_tensor", [bk, g], [tg], out=tg[:], in0=bk[:, :], in1=g[:], op=ALU.mult)
                            if bi == 1:
                                K.v("pool", "tensor_tensor", [tg, ac], [ac], out=ac[:], in0=ac[:], in1=tg[:], op=ALU.add)
                            else:
                                K.v("pool", "tensor_tensor", [tg, ac], [mb[i]], out=mslice, in0=ac[:], in1=tg[:], op=ALU.add)
                        gi += 1
            for i in range(n):
                m2 = mb[i]
                for g2 in range(2):
                    bk = K.bank()
                    bv = bf(bk)
                    for c in range(8):
                        cc = g2 * 8 + c
                        K.tr(bk, bv[:, c * 128:(c + 1) * 128], m2, m2[:, cc * 128:(cc + 1) * 128], C.ident)
                    K.v("act" if g2 == 0 else "dve", "copy" if g2 == 0 else "tensor_copy", [bk], [mT],
                        out=mT[:, g2 * 8:(g2 + 1) * 8, i * 128:(i + 1) * 128], in_=bv.rearrange("p (c t) -> p c t", c=8))
            for nb in range(4):
                w = wo[nb % 2]
                K.dma("pool", w[:], C.w_out[l].rearrange("(kc p) n -> p kc n", p=128)[:, :, nb * 512:(nb + 1) * 512], [], [w])
                for i, t in enumerate(tiles):
                    bk = K.bank()
                    for kc in range(16):
                        K.mm(bk, bk[:, :], mT, mT[:, kc, i * 128:(i + 1) * 128], w, w[:, kc, :], kc == 0, kc == 15)
                    h1 = hs[(nb * n + i) % 2]
                    o1 = ho[(nb * n + i) % 2]
                    G = Gr[1 if t < 2 else 0]
                    K.dma("sp", h1[:], src(t)[:, nb * 512:(nb + 1) * 512], [], [h1])
                    K.v("dve", "tensor_tensor", [bk, G], [o1], out=o1[:], in0=bk[:, :], in1=G[:, nb * 512:(nb + 1) * 512], op=ALU.mult)
                    K.v("dve", "tensor_tensor", [o1, h1], [o1], out=o1[:], in0=o1[:], in1=h1[:], op=ALU.add)
                    K.dma("act", dst(t)[:, nb * 512:(nb + 1) * 512], o1[:], [o1], [])


def phase_conv(K, C, l):
    with K.phase():
        cw = K.sb([128, 16, 5], F32, "cw")
        cb = K.sb([128, 16], F32, "cb")
        K.dma("sp", cw[:], C.convw[l], [], [cw])
        K.dma("sp", cb[:], C.convb[l], [], [cb])
        xp = K.ring(2, [128, 260 + 2052], F32, "xp")
        yy = K.ring(2, [128, NTOK], F32, "yy")
        yo = K.ring(2, [128, NTOK], F32, "yo")
        for b in xp:
            K.v("pool", "memset", [], [b], ap=b[:], constant=0.0)
        for cc in range(16):
            x = xp[cc % 2]
            y = yy[cc % 2]
            o = yo[cc % 2]
            K.dma("sp", x[:, 2:258], C.xbcT[cc * 128:(cc + 1) * 128, 0:256], [], [x])
            K.dma("sp", x[:, 262:262 + 2048], C.xbcT[cc * 128:(cc + 1) * 128, 256:NTOK], [], [x])
            for (eng, x0, y0, n) in (("dve", 0, 0, 256), ("dve", 260, 256, 2048)):
                K.v(eng, "tensor_scalar", [x, cw, cb], [y], out=y[:, y0:y0 + n], in0=x[:, x0:x0 + n], scalar1=cw[:, cc, 0:1],
                    scalar2=cb[:, cc:cc + 1], op0=ALU.mult, op1=ALU.add)
                for j in range(1, 5):
                    K.v(eng, "scalar_tensor_tensor", [x, cw, y], [y], out=y[:, y0:y0 + n], in0=x[:, x0 + j:x0 + j + n],
                        scalar=cw[:, cc, j:j + 1], in1=y[:, y0:y0 + n], op0=ALU.mult, op1=ALU.add)
            K.v("act", "activation", [y], [o], out=o[:], in_=y[:], func=AF.Silu)
            K.dma("sp", C.xbc2T[cc * 128:(cc + 1) * 128, :], o[:], [o], [])


def phase_ssd(K, C, l, need_ctx):
    with K.phase():
        K.rot = [0, 1, 2, 3, 4, 5]
        W = NS()
        W.sq = K.sb([128, 1536], F32, "sq")
        W.gst = K.sb([128, 24], F32, "gst")
        gn = K.sb([128, 1024], F32, "gn")
        W.gains = gn
        K.dma("sp", gn[:], bcast_rows(C.ssd_norm_g[l, :], 128), [], [gn])
        an = K.sb([128, 32], F32, "an")
        K.dma("sp", an[:], bcast_rows(C.ssd_a_log[l].rearrange("a h -> (a h)"), 128), [], [an])
        K.v("act", "activation", [an], [an], out=an[:], in_=an[:], func=AF.Exp)
        K.v("dve", "tensor_scalar", [an], [an], out=an[:], in0=an[:], scalar1=-1.0, scalar2=None, op0=ALU.mult)
        dsk = K.sb([128, 16], F32, "dsk")
        K.dma("sp", dsk[:], bcast_rows(C.ssd_d[l, :], 128), [], [dsk])
        msk = K.sb([128, 8, 128], F32, "msk")
        K.dma("sp", msk[:], C.cst[:, :, :], [], [msk])
        U, Lm, SL, SU = msk[:, 1, :], msk[:, 2, :], msk[:, 3, :], msk[:, 4, :]
        Sb_all = K.sb([128, 18, 1024], BF16, "Sball")
        Sf = K.sb([128, 1024], F32, "Sf")
        Sb = K.sb([128, 1024], F32, "Sb")
        Sfb = K.sb([128, 1024], BF16, "Sfb")
        xin = K.ring(2, [128, 16, 128], F32, "xin")
        dtc = K.ring(2, [128, 32], F32, "dtc")
        dta = K.ring(2, [128, 32], F32, "dta")
        xs = K.ring(2, [128, 1024], F32, "xs")
        Btm = K.ring(2, [128, 4, 128], BF16, "Btm")
        BT = K.ring(2, [128, 4, 128], BF16, "BT")
        CT = K.ring(2, [128, 4, 128], BF16, "CT")
        Yd = K.ring(2, [128, 16, 128], F32, "Yd")
        Md = K.ring(2, [128, 16, 128], BF16, "Md")
        CEd = K.ring(2, [128, 16, 128], BF16, "CEd")
        tD = K.ring(2, [128, 512], F32, "tD")
        tE = K.ring(2, [128, 512], F32, "tE")
        Gm = K.ring(2, [128, 4, 128], F32, "Gm")
        xdt = K.ring(2, [128, 16, 64], BF16, "xdt")
        wx = K.ring(2, [128, 16, 64], BF16, "wx")
        sm = K.ring(4, [128, 64], F32, "sm")
        zt = K.ring(2, [128, 1024], F32, "zt")
        v1 = K.ring(2, [128, 1024], F32, "v1")
        ob = K.ring(2, [128, 1024], BF16, "ob")
        stg = K.ring(2, [128, 8, 128], BF16, "stg")
        xv = C.xbc2T.rearrange("(cc p) t -> p cc t", p=128)
        cnt = [0]

        def prep(c, ncc):
            i = cnt[0]
            cnt[0] += 1
            x = xin[i % 2]
            K.dma("sp", x[:, 0:ncc, :], xv[:, 0:ncc, c * 128:(c + 1) * 128], [], [x])
            d = dtc[i % 2]
            K.dma("sp", d[:], C.dts[c * 128:(c + 1) * 128, :], [], [d])
            da = dta[i % 2]
            K.v("dve", "tensor_tensor", [d, an], [da], out=da[:], in0=d[:], in1=an[:], op=ALU.mult)
            xt = xs[i % 2]
            for g in range(2):
                bk = K.bank()
                for cc in range(4):
                    K.tr(bk, bk[:, cc * 128:(cc + 1) * 128], x, x[:, g * 4 + cc, :], C.identf)
                K.v("act", "copy", [bk], [xt], out=xt[:, g * 512:(g + 1) * 512], in_=bk[:, :])
            bt = Btm[i % 2]
            bk = K.bank()
            for g in range(4):
                K.tr(bk, bk[:, g * 128:(g + 1) * 128], x, x[:, 8 + g, :], C.identf)
            K.v("act", "copy", [bk], [bt], out=bt[:], in_=bk[:, :].rearrange("p (g n) -> p g n", g=4))
            return i, x, d, da, xt, bt

        def small_exp(lhsT_t, lhsT, da, d0):
            t = sm[cnt[0] % 4]
            cnt[0] += 1
            bk = K.bank()
            K.mm(bk, bk[:, 0:16], lhsT_t, lhsT, da, da[:, d0:d0 + 16], True, True)
            K.v("act", "activation", [bk], [t], out=t[:, 0:16], in_=bk[:, 0:16], func=AF.Exp)
            return t

        def state_update(St, Stb, bt, xt, d, da, d0, wmask_t, wmask, i):
            wv = small_exp(wmask_t, wmask, da, d0)
            dec = small_exp(C.onesf, C.onesf[:], da, d0)
            K.v("dve", "tensor_tensor", [wv, d], [wv], out=wv[:, 16:32], in0=wv[:, 0:16], in1=d[:, d0:d0 + 16], op=ALU.mult)
            w_ = wx[i % 2]
            K.v("dve", "tensor_tensor", [xt, wv], [w_], out=w_[:], in0=xt[:].rearrange("p (h q) -> p h q", h=16),
                in1=wv[:, 16:32].unsqueeze(2).to_broadcast([128, 16, 64]), op=ALU.mult)
            K.v("dve", "tensor_tensor", [St, dec], [St], out=St[:].rearrange("p (h q) -> p h q", h=16),
                in0=St[:].rearrange("p (h q) -> p h q", h=16), in1=dec[:, 0:16].unsqueeze(2).to_broadcast([128, 16, 64]), op=ALU.mult)
            for hf in range(2):
                bk = K.bank()
                for gg in range(2):
                    g = hf * 2 + gg
                    K.mm(bk, bk[:, gg * 256:(gg + 1) * 256], bt, bt[:, g, :], w_, w_[:, 4 * g:4 * g + 4, :].rearrange("p h q -> p (h q)"), True, True)
                K.v("dve", "tensor_tensor", [St, bk], [St], out=St[:, hf * 512:(hf + 1) * 512], in0=St[:, hf * 512:(hf + 1) * 512],
                    in1=bk[:, :], op=ALU.add)
            if Stb is not None:
                K.v("act", "copy", [St], [Stb], out=Stb[:], in_=St[:])

        K.v("pool", "memset", [], [Sb], ap=Sb[:], constant=0.0)
        K.v("pool", "memset", [], [Sf], ap=Sf[:], constant=0.0)
        K.v("pool", "memset", [], [Sfb], ap=Sfb[:], constant=0.0)
        for c in [1, 0] + list(range(17, 1, -1)):
            i, x, d, da, xt, bt = prep(c, 12)
            K.v("act", "copy", [Sb], [Sb_all], out=Sb_all[:, c, :], in_=Sb[:])
            state_update(Sb, None, bt, xt, d, da, 16, msk, SU, i)
        for c in range(18):
            i, x, d, da, xt, bt = prep(c, 16)
            emit_y = need_ctx or c >= 2
            if emit_y:
                z = zt[i % 2]
                K.dma("sp", z[:], C.zs[c * 128:(c + 1) * 128, :], [], [z])
                bT_, cT_ = BT[i % 2], CT[i % 2]
                K.v("act", "copy", [x], [bT_], out=bT_[:], in_=x[:, 8:12, :])
                K.v("act", "copy", [x], [cT_], out=cT_[:], in_=x[:, 12:16, :])
                bG = K.bank()
                for g in range(4):
                    K.mm(bG, bG[:, g * 128:(g + 1) * 128], bT_, bT_[:, g, :], cT_, cT_[:, g, :], True, True)
                for di in range(2):
                    K.v("dve", "tensor_tensor", [bG, msk], [Gm[di]], out=Gm[di][:], in0=bG[:, :].rearrange("p (g n) -> p g n", g=4),
                        in1=(U if di == 0 else Lm).unsqueeze(1).to_broadcast([128, 4, 128]), op=ALU.mult)
                Ms, CEs, xds = [], [], []
                for di in range(2):
                    d0 = 16 * di
                    mk = U if di == 0 else Lm
                    sl = SL if di == 0 else SU
                    gm = Gm[di]
                    Y = Yd[di]
                    K.v("pool", "tensor_tensor", [msk, da], [Y], out=Y[:], in0=mk.unsqueeze(1).to_broadcast([128, 16, 128]),
                        in1=da[:, d0:d0 + 16].unsqueeze(2).to_broadcast([128, 16, 128]), op=ALU.mult)
                    M = Md[di]
                    CE = CEd[di]
                    for hg in range(4):
                        yv = Y[:, 4 * hg:4 * hg + 4, :].rearrange("p h n -> p (h n)")
                        b1 = K.bank()
                        K.mm(b1, b1[:, :], msk, sl, Y, yv, True, True)
                        t1 = tD[hg % 2]
                        K.v("act", "activation", [b1], [t1], out=t1[:], in_=b1[:, :], func=AF.Exp)
                        K.v("dve", "tensor_tensor", [t1, gm], [M], out=M[:, 4 * hg:4 * hg + 4, :], in0=t1[:].rearrange("p (h n) -> p h n", h=4),
                            in1=gm[:, hg, :].unsqueeze(1).to_broadcast([128, 4, 128]), op=ALU.mult)
                        b2 = K.bank()
                        K.mm(b2, b2[:, :], C.onesf, C.onesf[:], Y, yv, True, True)
                        t2 = tE[hg % 2]
                        K.v("act", "activation", [b2], [t2], out=t2[:], in_=b2[:, :], func=AF.Exp)
                        K.v("pool", "tensor_tensor", [t2, cT_], [CE], out=CE[:, 4 * hg:4 * hg + 4, :], in0=t2[:].rearrange("p (h n) -> p h n", h=4),
                            in1=cT_[:, hg, :].unsqueeze(1).to_broadcast([128, 4, 128]), op=ALU.mult)
                    xd = xdt[di]
                    K.v("dve", "tensor_tensor", [xt, d], [xd], out=xd[:], in0=xt[:].rearrange("p (h q) -> p h q", h=16),
                        in1=d[:, d0:d0 + 16].unsqueeze(2).to_broadcast([128, 16, 64]), op=ALU.mult)
                    Ms.append(M)
                    CEs.append(CE)
                    xds.append(xd)
                yb = [K.pb[6], K.pb[7]]
                for h in range(16):
                    bk = yb[h // 8]
                    o = bk[:, (h % 8) * 64:(h % 8 + 1) * 64]
                    K.mm(bk, o, Ms[0], Ms[0][:, h, :], xds[0], xds[0][:, h, :], True, False)
                    K.mm(bk, o, Ms[1], Ms[1][:, h, :], xds[1], xds[1][:, h, :], False, False)
                    K.mm(bk, o, CEs[0], CEs[0][:, h, :], Sfb, Sfb[:, h * 64:(h + 1) * 64], False, False)
                    K.mm(bk, o, CEs[1], CEs[1][:, h, :], Sb_all, Sb_all[:, c, h * 64:(h + 1) * 64], False, True)
                v = v1[i % 2]
                K.v("dve", "tensor_tensor", [xt, dsk], [v], out=v[:].rearrange("p (h q) -> p h q", h=16),
                    in0=xt[:].rearrange("p (h q) -> p h q", h=16), in1=dsk[:].unsqueeze(2).to_broadcast([128, 16, 64]), op=ALU.mult)
                for hf in range(2):
                    K.v("dve", "tensor_tensor", [v, yb[hf]], [v], out=v[:, hf * 512:(hf + 1) * 512], in0=v[:, hf * 512:(hf + 1) * 512],
                        in1=yb[hf][:, :], op=ALU.add)
                if C.dbgv is not None:
                    K.dma("sp", C.dbgv[c * 128:(c + 1) * 128, :], v[:], [v], [])
                K.v("pool", "tensor_tensor", [v, z], [v], out=v[:], in0=v[:], in1=z[:], op=ALU.mult)
                o_ = ob[i % 2]
                rms_groups(K, v, v[:].rearrange("p (g q) -> p g q", g=4), 4, 256, gn[:].rearrange("p (g q) -> p g q", g=4),
                           o_, o_[:].rearrange("p (g q) -> p g q", g=4), W)
                bk = K.bank()
                bv = bf(bk)
                for cc in range(8):
                    K.tr(bk, bv[:, cc * 128:(cc + 1) * 128], o_, o_[:, cc * 128:(cc + 1) * 128], C.ident)
                sg_ = stg[i % 2]
                K.v("act", "copy", [bk], [sg_], out=sg_[:], in_=bv.rearrange("p (c t) -> p c t", c=8))
                K.dma("sp", C.brT[1].rearrange("(kc p) t -> p kc t", p=128)[:, :, c * 128:(c + 1) * 128], sg_[:], [sg_], [])
            if c < 17:
                state_update(Sf, Sfb, bt, xt, d, da, 0, msk, SL, i)
        if C.dbgv is not None:
            K.dma("sp", C.dbgS[:, :], Sb_all[:].rearrange("p c f -> p (c f)"), [Sb_all], [])

def tile_rows(ap2d, t):
    return ap2d[t * 128:(t + 1) * 128, :]


def build(dbg=(), stop_after=None):
    nc = bass.Bass("TRN2", target_bir_lowering=False)
    C = NS()

    def inp(name, shape, dt=F32):
        return nc.dram_tensor(name, list(shape), dt, kind="ExternalInput").ap()

    C.x = inp("x", [T, D])
    C.ctx = inp("ctx", [TC, D])
    C.cT = inp("cT", [128, 16, 2])
    C.cst = inp("cst", [128, 8, 128])
    C.w_ada = inp("w_ada", [L, D, 9 * D])
    C.b_ada = inp("b_ada", [L, 9 * D])
    C.norm_g = inp("norm_g", [L, 3, D])
    C.ffn_w = {}
    for nm in ("ffn1", "ffn2"):
        C.ffn_w[nm] = (inp(nm + "_w_gate", [L, D, DFF]), inp(nm + "_w_up", [L, D, DFF]), inp(nm + "_w_down", [L, DFF, D]))
    C.w_in = inp("w_in", [L, D, DIN])
    C.mla_kv_norm_g = inp("mla_kv_norm_g", [L, 512])
    C.mla_w_uk = inp("mla_w_uk", [L, 512, 1024])
    C.mla_w_uv = inp("mla_w_uv", [L, 512, 1024])
    C.mla_q_norm_g = inp("mla_q_norm_g", [L, 192])
    C.mla_k_norm_g = inp("mla_k_norm_g", [L, 192])
    C.ssd_dt_bias = inp("ssd_dt_bias", [L, 2, 16])
    C.na_q_norm_g = inp("na_q_norm_g", [L, 128])
    C.na_k_norm_g = inp("na_k_norm_g", [L, 128])
    C.rope = inp("rope", [T, 64])
    C.nab = inp("nab", [L, 8, 25, 128, 128])
    C.convw = inp("convw", [L, 128, 16, 5])
    C.convb = inp("convb", [L, 128, 16])
    C.ssd_a_log = inp("ssd_a_log", [L, 2, 16])
    C.ssd_d = inp("ssd_d", [L, 16])
    C.ssd_norm_g = inp("ssd_norm_g", [L, 1024])
    C.w_branch = inp("w_branch", [L, 3, 1024, D])
    C.w_out = inp("w_out", [L, D, D])
    out = nc.dram_tensor("out", [T, D], F32, kind="ExternalOutput").ap()

    with ExitStack() as es:
        K = KB(nc, es, dbg)
        C.modr = [K.dram("modr%d" % l, [2, 9 * D], F32) for l in range(L)]
        hA = K.dram("hA", [NTOK, D], F32)
        hB = K.dram("hB", [NTOK, D], F32)
        C.qT = K.dram("qT", [8, 192, NTOK], BF16)
        C.kT = K.dram("kT", [8, 192, NTOK], BF16)
        C.vm = K.dram("vm", [NTOK, 1024], BF16)
        C.nqT = K.dram("nqT", [8, 128, NTOK], BF16)
        C.nkT = K.dram("nkT", [8, 128, NTOK], BF16)
        C.nv = K.dram("nv", [NTOK, 1024], BF16)
        C.xbcT = K.dram("xbcT", [2048, NTOK], F32)
        C.zs = K.dram("zs", [NTOK, 1024], F32)
        C.xbc2T = K.dram("xbc2T", [2048, NTOK], F32)
        C.dbgv = K.dram("dbgv", [NTOK, 1024], F32) if "dbgv" in K.dbg else None
        C.dbgS = K.dram("dbgS", [128, 18 * 1024], BF16, kind="ExternalOutput") if "dbgv" in K.dbg else None
        C.dts = K.dram("dts", [NTOK, 32], F32)
        C.gts = K.dram("gts", [NTOK, 6144], F32)
        C.brT = K.dram("brT", [3, 1024, NTOK], BF16)

        def done(tag):
            return stop_after == tag

        phase_consts(K, C)
        for l in range(L):
            phase_adaln(K, C, l)

        def src0(t):
            return tile_rows(C.ctx, t) if t < 2 else tile_rows(C.x, t - 2)

        def rA(t):
            return tile_rows(hA, t)

        def rB(t):
            return tile_rows(hB, t)

        def rOut(t):
            return tile_rows(out, t - 2)
        lat_blocks = [list(range(2, 8)), list(range(8, 14)), list(range(14, 18))]
        blocks_all = [list(range(0, 6)), list(range(6, 12)), list(range(12, 18))]
        for l in range(L):
            need_ctx = l < L - 1
            if l == 0:
                f1s, f1d, mgd, f2d = src0, rA, rB, rA
            else:
                f1s, f1d, mgd, f2d = rA, rB, rA, rOut
            wg, wu, wd = C.ffn_w["ffn1"]
            phase_ffn(K, C, l, 0, wg[l], wu[l], wd[l], blocks_all, f1s, f1d)
            phase_win(K, C, l, blocks_all, f1d)
            phase_mla(K, C, l, need_ctx)
            phase_na(K, C, l, need_ctx)
            phase_conv(K, C, l)
            phase_ssd(K, C, l, need_ctx)
            blk2 = blocks_all if need_ctx else lat_blocks
            phase_merge(K, C, l, blk2, f1d, mgd)
            wg, wu, wd = C.ffn_w["ffn2"]
            phase_ffn(K, C, l, 2, wg[l], wu[l], wd[l], blk2, mgd, f2d)
            if stop_after == "l0" and l == 0:
                break
        K.S.finish()
        print("total ops", K.S.total)
    return nc


def host_consts():
    cst = np.zeros((128, 8, 128), np.float32)
    cst[:, 0, :] = np.eye(128, dtype=np.float32)
    k = np.arange(128)[:, None]
    j = np.arange(128)[None, :]
    cst[:, 1, :] = (k <= j)
    cst[:, 2, :] = (k >= j)
    cst[:, 3, :] = (k > j)
    cst[:, 4, :] = (k < j)
    return cst


def na_bias_layout(rpb):
    Ln = rpb.shape[0]
    out = np.full((Ln, 8, 5, 5, 128, 128), NEG, np.float32)
    rep = [0, 1, 5, 14, 15]
    kk = np.arange(640)
    krl = kk // 64
    kc = kk % 64
    ql = np.arange(128)
    qrl = ql // 64
    qc = ql % 64
    for v, i in enumerate(rep):
        kb10 = min(max(2 * i - 4, 0), 22)
        krow = kb10 + krl
        r = 2 * i + qrl
        ws = np.clip(r - 4, 0, 24)
        wc = np.clip(qc - 8, 0, 48)
        okr = (krow[:, None] >= ws[None, :]) & (krow[:, None] < ws[None, :] + 8)
        okc = (kc[:, None] >= wc[None, :]) & (kc[:, None] < wc[None, :] + 16)
        ok = okr & okc
        ro = np.clip(krow[:, None] - r[None, :] + 7, 0, 14)
        co = np.clip(kc[:, None] - qc[None, :] + 15, 0, 30)
        g = rpb[:, :, ro, co]
        g = np.where(ok[None, None], g, np.float32(NEG)).astype(np.float32)
        out[:, :, v] = g.reshape(Ln, 8, 5, 128, 128)
    return np.ascontiguousarray(out.reshape(Ln, 8, 25, 128, 128))


def rope_table():
    t = np.arange(T)
    pos = np.stack([t // 64, t % 64], axis=-1).astype(np.float32)
    inv = (10000.0 ** (-np.arange(16, dtype=np.float32) / 16)).astype(np.float32)
    ang = pos[:, :, None] * inv
    return np.concatenate([np.cos(ang).reshape(T, 32), np.sin(ang).reshape(T, 32)], axis=1).astype(np.float32)


def make_in_maps(inputs, ncores=8):
    cst = host_consts()
    rope = rope_table()
    nab = na_bias_layout(inputs["na_rpb"])
    maps = []
    for b in range(ncores):
        cv = np.stack([inputs["c"][b], inputs["c_ctx"]], axis=0)
        cT = np.ascontiguousarray(cv.reshape(2, 16, 128).transpose(2, 1, 0))
        m = {"x": np.ascontiguousarray(inputs["x"][b]), "ctx": np.ascontiguousarray(inputs["ctx"][b]),
             "cT": cT, "cst": cst}
        for k in ("w_ada", "b_ada", "norm_g", "ffn1_w_gate", "ffn1_w_up", "ffn1_w_down",
                  "ffn2_w_gate", "ffn2_w_up", "ffn2_w_down", "w_in", "mla_kv_norm_g", "mla_w_uk", "mla_w_uv",
                  "mla_q_norm_g", "mla_k_norm_g", "ssd_dt_bias", "na_q_norm_g", "na_k_norm_g"):
            m[k] = inputs[k]
        m["rope"] = rope
        m["nab"] = nab
        m["convw"] = np.ascontiguousarray(inputs["ssd_conv_w"].reshape(L, 5, 16, 128).transpose(0, 3, 2, 1))
        m["convb"] = np.ascontiguousarray(inputs["ssd_conv_b"].reshape(L, 16, 128).transpose(0, 2, 1))
        for k in ("ssd_a_log", "ssd_d", "ssd_norm_g"):
            m[k] = inputs[k]
        m["w_branch"] = inputs["w_branch"]
        m["w_out"] = inputs["w_out"]
        maps.append(m)
    return maps


def kernel(**inputs):
    inputs = {k: np.asarray(v) for k, v in inputs.items()}
    nc = build()
    maps = make_in_maps(inputs, 8)
    res = run_bass_kernel_spmd(nc, maps, core_ids=list(range(8)))
    return np.stack([r["out"] for r in res.results], axis=0)
```
